# Optimizing a Trainium2 kernel written in Bass

```python
import math
import jax, jax.numpy as jnp
from jax import lax
import numpy as np

D_MODEL = 1024
BATCH = 8
SEQ = 4096
DEPTH = 4

PLE_DIM = 256
CHUNK = 128
A_WIDTH = D_MODEL
A_GROUPS = 8
A_GROUP_DIM = A_WIDTH // A_GROUPS
B_HEAD_DIM = 64
B_HEADS = D_MODEL // (2 * B_HEAD_DIM)
B_WIDTH = B_HEADS * 2 * B_HEAD_DIM
Q_BLOCK = 128
EPS = 1e-6
NEG = -1e30
IN_WIDTHS = (A_WIDTH, A_WIDTH, A_WIDTH, B_WIDTH, B_WIDTH, B_WIDTH, B_WIDTH, D_MODEL, D_MODEL)
IN_TOTAL = sum(IN_WIDTHS)
SPLIT_POINTS = tuple(int(c) for c in np.cumsum(IN_WIDTHS)[:-1])

kernel_name = "hybrid_gmlp_diffattn_gated_block"


def rmsnorm(x, g):
    xf = x.astype(jnp.float32)
    y = xf * lax.rsqrt(jnp.mean(xf * xf, axis=-1, keepdims=True) + EPS)
    return (y * g.astype(jnp.float32)).astype(x.dtype)


def layernorm(x, g, b):
    xf = x.astype(jnp.float32)
    mu = jnp.mean(xf, axis=-1, keepdims=True)
    xc = xf - mu
    y = xc * lax.rsqrt(jnp.mean(xc * xc, axis=-1, keepdims=True) + EPS)
    return (y * g.astype(jnp.float32) + b.astype(jnp.float32)).astype(x.dtype)


def lambda_init(layer_idx):
    return 0.8 - 0.6 * math.exp(-0.3 * layer_idx)


def gmlp_branch(u, v, z, ln_g, ln_b, ws, bs):
    B, S, _ = v.shape
    n_chunks = S // CHUNK
    u = jax.nn.gelu(u, approximate=False)
    v = layernorm(jax.nn.gelu(v, approximate=False), ln_g, ln_b)
    vc = v.reshape(B, n_chunks, CHUNK, A_GROUPS, A_GROUP_DIM)
    w = jnp.tril(ws)
    y = jnp.einsum('gts,bcsgd->bctgd', w, vc) + bs.T[None, None, :, :, None]
    y = y.reshape(B, S, A_WIDTH)
    return u * y * jax.nn.silu(z)


def diff_attn_branch(q, k, v, z, lam_q, lam_k, subln_g, lam_init):
    B, S, _ = q.shape
    q = q.reshape(B, S, B_HEADS, 2, B_HEAD_DIM)
    k = k.reshape(B, S, B_HEADS, 2, B_HEAD_DIM)
    v = v.reshape(B, S, B_HEADS, 2 * B_HEAD_DIM)
    lq = lam_q.astype(jnp.float32)
    lk = lam_k.astype(jnp.float32)
    lam = jnp.exp(jnp.sum(lq[0] * lk[0])) - jnp.exp(jnp.sum(lq[1] * lk[1])) + lam_init
    scale = B_HEAD_DIM ** -0.5
    n_q = S // Q_BLOCK
    q_blocks = q.reshape(B, n_q, Q_BLOCK, B_HEADS, 2, B_HEAD_DIM).transpose(1, 0, 2, 3, 4, 5)
    k_pos = jnp.arange(S)

    def one_block(args):
        q_blk, qi = args
        s = jnp.einsum('bqhmd,bkhmd->bhmqk', q_blk, k).astype(jnp.float32) * scale
        q_pos = qi * Q_BLOCK + jnp.arange(Q_BLOCK)
        s = jnp.where((q_pos[:, None] >= k_pos[None, :])[None, None, None], s, NEG)
        pm = jax.nn.softmax(s, axis=-1)
        a = pm[:, :, 0] - lam * pm[:, :, 1]
        return jnp.einsum('bhqk,bkhe->bqhe', a.astype(v.dtype), v)

    o = lax.map(one_block, (q_blocks, jnp.arange(n_q)))
    o = o.transpose(1, 0, 2, 3, 4).reshape(B, S, B_HEADS, 2 * B_HEAD_DIM)
    o = rmsnorm(o, subln_g) * (1.0 - lam_init)
    o = o.reshape(B, S, B_WIDTH)
    return o * jax.nn.silu(z)


def setup_inputs(seed: int = 0) -> dict:
    key = jax.random.key(seed)
    ks = jax.random.split(key, 20)

    def nrm(k, shape, scale):
        return jax.random.normal(k, shape, jnp.float32) * scale

    return {
        "x": nrm(ks[0], (BATCH, SEQ, D_MODEL), 1.0),
        "p": nrm(ks[1], (DEPTH, BATCH, SEQ, PLE_DIM), 1.0),
        "norm_g": 1.0 + nrm(ks[2], (DEPTH, D_MODEL), 0.05),
        "w_in": nrm(ks[3], (DEPTH, D_MODEL, IN_TOTAL), D_MODEL ** -0.5),
        "a_ln_g": 1.0 + nrm(ks[4], (DEPTH, A_WIDTH), 0.05),
        "a_ln_b": nrm(ks[5], (DEPTH, A_WIDTH), 0.02),
        "a_ws": nrm(ks[6], (DEPTH, A_GROUPS, CHUNK, CHUNK), CHUNK ** -0.5),
        "a_bs": 1.0 + nrm(ks[7], (DEPTH, A_GROUPS, CHUNK), 0.05),
        "lam_q": nrm(ks[8], (DEPTH, 2, B_HEAD_DIM), 0.1),
        "lam_k": nrm(ks[9], (DEPTH, 2, B_HEAD_DIM), 0.1),
        "subln_g": 1.0 + nrm(ks[10], (DEPTH, 2 * B_HEAD_DIM), 0.05),
        "w_a_out": nrm(ks[11], (DEPTH, A_WIDTH, D_MODEL), A_WIDTH ** -0.5),
        "w_b_out": nrm(ks[12], (DEPTH, B_WIDTH, D_MODEL), B_WIDTH ** -0.5),
        "w_o": nrm(ks[13], (DEPTH, D_MODEL, D_MODEL), D_MODEL ** -0.5),
        "ple_norm_g": 1.0 + nrm(ks[14], (DEPTH, D_MODEL), 0.05),
        "w_ple": nrm(ks[15], (DEPTH, PLE_DIM, D_MODEL), PLE_DIM ** -0.5),
        "w_ple_gate": nrm(ks[16], (DEPTH, D_MODEL, D_MODEL), D_MODEL ** -0.5),
        "final_g": 1.0 + nrm(ks[17], (D_MODEL,), 0.05),
    }


def reference(x, p, norm_g, w_in, a_ln_g, a_ln_b, a_ws, a_bs, lam_q, lam_k, subln_g,
              w_a_out, w_b_out, w_o, ple_norm_g, w_ple, w_ple_gate, final_g):
    for i in range(DEPTH):
        h = rmsnorm(x, norm_g[i])
        proj = h @ w_in[i]
        u_a, v_a, z_a, q, k, v_b, z_b, g_a, g_b = jnp.split(proj, SPLIT_POINTS, axis=-1)
        y_a = gmlp_branch(u_a, v_a, z_a, a_ln_g[i], a_ln_b[i], a_ws[i], a_bs[i]) @ w_a_out[i]
        y_b = diff_attn_branch(q, k, v_b, z_b, lam_q[i], lam_k[i], subln_g[i],
                               lambda_init(i)) @ w_b_out[i]
        merged = jax.nn.sigmoid(g_a) * y_a + jax.nn.sigmoid(g_b) * y_b
        x = x + merged @ w_o[i]
        ple_gate = jax.nn.sigmoid(rmsnorm(x, ple_norm_g[i]) @ w_ple_gate[i])
        x = x + (p[i] @ w_ple[i]) * ple_gate
    return rmsnorm(x, final_g)
```

```python
import math
import numpy as np
import concourse.bass as bass
import concourse.mybir as mybir
from concourse.bass_utils import run_bass_kernel_spmd

F32 = mybir.dt.float32
BF16 = mybir.dt.bfloat16
AF = mybir.ActivationFunctionType
ALU = mybir.AluOpType
AX = mybir.AxisListType

P = 128
D = 1024
NCH = 8
INW = 9216
PLE = 256
EPS = 1e-6
C_U, C_VA, C_ZA, C_Q, C_K, C_VB, C_ZB, C_GA, C_GB = [i * 1024 for i in range(9)]
N_CORES = 8


def lambda_init(i):
    return 0.8 - 0.6 * math.exp(-0.3 * i)


class _Op:
    __slots__ = ("eng", "fn", "reads", "writes", "dkey", "deps", "sig", "event", "barrier")

    def __init__(self, eng, fn, reads, writes, dkey):
        self.eng = eng
        self.fn = fn
        self.reads = reads
        self.writes = writes
        self.dkey = dkey
        self.deps = ()
        self.sig = False
        self.event = None
        self.barrier = False


class Tracker:
    ENGS = ("pe", "act", "dve", "pool", "sp")

    def __init__(self, nc):
        self.nc = nc
        self.ops = []
        self.eng_obj = {"pe": nc.tensor, "act": nc.scalar, "dve": nc.vector,
                        "pool": nc.gpsimd, "sp": nc.sync}

    def op(self, eng, fn, reads=(), writes=(), dkey=None):
        self.ops.append(_Op(eng, fn, tuple(reads), tuple(writes), dkey))

    def barrier(self):
        for e in self.ENGS:
            o = _Op(e, None, (), (), None)
            o.barrier = True
            self.ops.append(o)

    def finalize(self):
        ops = self.ops
        last_w = {}
        readers = {}
        last_eng = {}
        last_key = {}
        i = 0
        n = len(ops)
        while i < n:
            op = ops[i]
            if op.barrier:
                deps = set(last_eng.values()) | set(last_key.values())
                j = i
                while j < n and ops[j].barrier:
                    ops[j].deps = tuple(deps)
                    j += 1
                for d in deps:
                    if ops[d].dkey is None:
                        ops[d].sig = True
                last_w = {}
                readers = {}
                i = j
                continue
            deps = set()
            for r in op.reads:
                w = last_w.get(r)
                if w is not None:
                    deps.add(w)
            for wr in op.writes:
                w = last_w.get(wr)
                if w is not None:
                    deps.add(w)
                rd = readers.get(wr)
                if rd:
                    deps.update(rd.values())
            rk = op.dkey if op.dkey is not None else op.eng
            for r in op.reads:
                readers.setdefault(r, {})[rk] = i
            for wr in op.writes:
                last_w[wr] = i
                readers[wr] = {}
            deps.discard(i)
            if op.eng == "pe" and op.dkey is None:
                deps = {d for d in deps if not (ops[d].eng == "pe" and ops[d].dkey is None)}
            op.deps = tuple(deps)
            for d in deps:
                if ops[d].dkey is None:
                    ops[d].sig = True
            if op.dkey is not None:
                last_key[op.dkey] = i
            else:
                last_eng[op.eng] = i
            i += 1

        nc = self.nc
        keys = []
        for op in ops:
            if op.dkey is not None and op.dkey not in keys:
                keys.append(op.dkey)
        sems = {}
        for e in self.ENGS:
            sems[e] = nc.alloc_semaphore("prog_" + e)
        for k in keys:
            sems[k] = nc.alloc_semaphore("dma_" + str(len(sems)))
        cnt = {k: 0 for k in sems}
        seen = {e: {} for e in self.ENGS}
        for op in ops:
            eo = self.eng_obj[op.eng]
            need = {}
            for d in op.deps:
                ev = ops[d].event
                if ev is None:
                    raise RuntimeError("dependency on unsignaled op")
                if need.get(ev[0], 0) < ev[1]:
                    need[ev[0]] = ev[1]
            sn = seen[op.eng]
            for k, v in need.items():
                if sn.get(k, 0) < v:
                    eo.wait_ge(sems[k], v)
                    sn[k] = v
            if op.fn is None:
                continue
            ins = op.fn()
            if op.dkey is not None:
                cnt[op.dkey] += 16
                ins.then_inc(sems[op.dkey], 16)
                op.event = (op.dkey, cnt[op.dkey])
            elif op.sig:
                cnt[op.eng] += 1
                ins.then_inc(sems[op.eng], 1)
                op.event = (op.eng, cnt[op.eng])
        fin = {}
        for k in keys:
            if cnt[k] > 0:
                fin[k] = cnt[k]
        for k, v in fin.items():
            if seen["sp"].get(k, 0) < v:
                nc.sync.wait_ge(sems[k], v)
        self.n_ops = len(ops)


class Arena:
    def __init__(self, nc, nbytes):
        self.n16 = nbytes // 2
        self.t = nc.alloc_sbuf_tensor("arena", [P, self.n16], BF16)
        self.off = 0

    def mark(self):
        return self.off

    def reset(self, m):
        self.off = m

    def alloc(self, nelem, dtype):
        n16 = nelem * (2 if dtype == F32 else 1)
        n16 = (n16 + 15) // 16 * 16
        assert self.off + n16 <= self.n16, f"arena overflow {self.off + n16} > {self.n16}"
        ap = self.t[:, self.off:self.off + n16]
        self.off += n16
        if dtype == F32:
            return ap.bitcast(F32)[:, 0:nelem]
        return ap[:, 0:nelem]


def build_program(S, NL, TBA=1024, last_final=True):
    assert S % 512 == 0
    TBA = min(TBA, S)
    nc = bass.Bass("TRN2", target_bir_lowering=False)
    T = Tracker(nc)
    NT = S // P
    NQB = S // 512
    NHB = 8

    def din(name, shape, dt=F32):
        return nc.dram_tensor(name, shape, dt, kind="ExternalInput").ap()

    def dscr(name, shape, dt):
        return nc.dram_tensor(name, shape, dt, kind="Internal").ap()

    x_in = din("x", [S, D])
    p_in = din("p", [NL, S, PLE])
    norm_g = din("norm_g", [NL, D])
    w_in = din("w_in", [NL, D, INW])
    a_ln_g = din("a_ln_g", [NL, D])
    a_ln_b = din("a_ln_b", [NL, D])
    a_ws = din("a_ws", [NL, 8, P, P])
    a_bs = din("a_bs", [NL, 8, P])
    lam_q = din("lam_q", [NL, P])
    lam_k = din("lam_k", [NL, P])
    subln_g = din("subln_g", [NL, P])
    w_a_out = din("w_a_out", [NL, D, D])
    w_b_out = din("w_b_out", [NL, D, D])
    w_o = din("w_o", [NL, D, D])
    ple_norm_g = din("ple_norm_g", [NL, D])
    w_ple = din("w_ple", [NL, PLE, D])
    w_ple_gate = din("w_ple_gate", [NL, D, D])
    final_g = din("final_g", [1, D])
    consts = din("consts", [P, 4 * P])
    out = nc.dram_tensor("out", [S, D], F32, kind="ExternalOutput").ap()

    wb_in = dscr("wb_in", [NL, D, INW], BF16)
    wb_a = dscr("wb_a", [NL, D, D], BF16)
    wb_b = dscr("wb_b", [NL, D, D], BF16)
    wb_o = dscr("wb_o", [NL, D, D], BF16)
    wb_pg = dscr("wb_pg", [NL, D, D], BF16)
    wb_ple = dscr("wb_ple", [NL, PLE, D], BF16)
    xs = dscr("xs", [S, D], F32)
    QT = dscr("QT", [8, P, S], BF16)
    KT = dscr("KT", [8, P, S], BF16)
    VS = dscr("VS", [S, D], BF16)
    ZBs = dscr("ZBs", [8, P, S], BF16)
    THB = dscr("THB", [8, P, S], BF16)
    GAYA = dscr("GAYA", [8, P, S], BF16)
    OBT = dscr("OBT", [8, P, S], BF16)

    ar = Arena(nc, 206 * 1024)
    ps = nc.alloc_psum_tensor("ps", [P, 8, 512], F32)

    eng = T.eng_obj
    pe, act, dve, pool, sp = eng["pe"], eng["act"], eng["dve"], eng["pool"], eng["sp"]

    def DMA(out_ap, in_ap, reads, writes, dkey, q="sp", **kw):
        e = eng[q]
        T.op(q, lambda: e.dma_start(out=out_ap, in_=in_ap, **kw), reads, writes, dkey)

    def MM(out_ap, lhsT, rhs, start, stop, reads, writes):
        T.op("pe", lambda: pe.matmul(out_ap, lhsT=lhsT, rhs=rhs, start=start, stop=stop,
                                     skip_group_check=True), reads, writes)

    def TR(out_ap, in_ap, ident, reads, writes):
        T.op("pe", lambda: pe.transpose(out_ap, in_ap, ident), reads, writes)

    def ACT(out_ap, in_ap, func, reads, writes, scale=None, accum=None):
        kw = {}
        if scale is not None:
            kw["scale"] = scale
        if accum is not None:
            kw["accum_out"] = accum
        T.op("act", lambda: act.activation(out=out_ap, in_=in_ap, func=func, **kw), reads, writes)

    def OP(e, name, reads, writes, **kw):
        eo = eng[e]
        T.op(e, lambda: getattr(eo, name)(**kw), reads, writes)

    def TS(e, out_ap, in0, s1, s2, op0, op1, reads, writes):
        eo = eng[e]
        if op1 is None:
            T.op(e, lambda: eo.tensor_scalar(out=out_ap, in0=in0, scalar1=s1, scalar2=None, op0=op0),
                 reads, writes)
        else:
            T.op(e, lambda: eo.tensor_scalar(out=out_ap, in0=in0, scalar1=s1, scalar2=s2, op0=op0, op1=op1),
                 reads, writes)

    def STT(out_ap, in0, scalar, in1, op0, op1, reads, writes):
        T.op("dve", lambda: dve.scalar_tensor_tensor(out=out_ap, in0=in0, scalar=scalar, in1=in1,
                                                     op0=op0, op1=op1), reads, writes)

    def TT(e, out_ap, in0, in1, op, reads, writes):
        eo = eng[e]
        T.op(e, lambda: eo.tensor_tensor(out=out_ap, in0=in0, in1=in1, op=op), reads, writes)

    def COPY(e, out_ap, in_ap, reads, writes):
        if e == "act":
            ACT(out_ap, in_ap, AF.Copy, reads, writes)
        else:
            eo = eng[e]
            T.op(e, lambda: eo.tensor_copy(out=out_ap, in_=in_ap), reads, writes)

    def MEMSET(e, ap, val, writes):
        eo = eng[e]
        T.op(e, lambda: eo.memset(ap, val), (), writes)

    def rsqrt_pool(out_ap, in_ap, mul, add, tmp_ap, reads, writes, tmpres):
        TS("pool", tmp_ap, in_ap, mul, add, ALU.mult, ALU.add, reads, [tmpres])
        TT("pool", out_ap, tmp_ap, neghalf[:, 0:tmp_ap.shape[1]], ALU.pow, [tmpres, "neghalf"], writes)

    cst_f = ar.alloc(4 * P, F32)
    ident = ar.alloc(P, BF16)
    maskneg = ar.alloc(P, BF16)
    zeros = ar.alloc(P, BF16)
    tril01 = cst_f[:, 2 * P:3 * P]
    neghalf = ar.alloc(16, F32)
    fgB = ar.alloc(D, F32)
    gB = ar.alloc(D, F32)
    plgB = ar.alloc(D, F32)
    lngB = ar.alloc(D, F32)
    biasH = ar.alloc(D, F32)
    WsT = ar.alloc(8 * P, BF16)
    sgcol = ar.alloc(1, F32)
    neglam = ar.alloc(1, F32)
    small = ar.alloc(64, F32)
    persist_mark = ar.mark()

    DMA(cst_f, consts, [], ["cst_f"], "k_cst")
    COPY("dve", ident, cst_f[:, 0:P], ["cst_f"], ["ident"])
    COPY("dve", maskneg, cst_f[:, P:2 * P], ["cst_f"], ["maskneg"])
    MEMSET("dve", zeros, 0.0, ["zeros"])
    MEMSET("dve", neghalf, -0.5, ["neghalf"])
    DMA(fgB, final_g[0, :].partition_broadcast(P), [], ["fgB"], "k_fg")

    def issue_casts(l):
        key = ("cast", l)
        for c in range(8):
            DMA(wb_in[l, c * P:(c + 1) * P, :], w_in[l, c * P:(c + 1) * P, :], [], [], key,
                q="pool", max_dma_last_dim=4096)
        for src, dst in ((w_a_out, wb_a), (w_b_out, wb_b), (w_o, wb_o), (w_ple_gate, wb_pg)):
            for c in range(2):
                DMA(dst[l, c * 512:(c + 1) * 512, :], src[l, c * 512:(c + 1) * 512, :], [], [], key,
                    q="pool", max_dma_last_dim=4096)
        DMA(wb_ple[l], w_ple[l], [], [("wb", l)], key, q="pool", max_dma_last_dim=4096)

    issue_casts(0)

    def psb(b0, nb=1):
        if nb == 1:
            return ps[:, b0, :]
        return ps[:, b0:b0 + nb, :].rearrange("p b n -> p (b n)")

    def psb16(b0):
        return ps[:, b0, :].bitcast(BF16)

    for l in range(NL):
        x_src = x_in if l == 0 else xs
        is_last = (l == NL - 1)
        li = lambda_init(l)

        T.barrier()
        ar.reset(persist_mark)
        pm = ar.mark()
        DMA(gB, norm_g[l, :].partition_broadcast(P), [], ["gB"], "k_gB")
        DMA(plgB, ple_norm_g[l, :].partition_broadcast(P), [], ["plgB"], "k_plgB")
        DMA(lngB, a_ln_g[l, :].partition_broadcast(P), [], ["lngB"], "k_lngB")
        TS("dve", lngB, lngB, 0.5, None, ALU.mult, None, ["lngB"], ["lngB"])
        lnbB = ar.alloc(D, F32)
        DMA(lnbB, a_ln_b[l, :].partition_broadcast(P), [], ["lnbB"], "k_lnbB")
        ws_tok = ar.alloc(8 * P, F32)
        ws3 = ws_tok.rearrange("p (g s) -> p g s", g=8)
        DMA(ws3, a_ws[l].rearrange("g t s -> t g s"), [], ["ws_tok"], "k_ws")
        bs_tok = small[:, 0:8]
        rw_tok = small[:, 8:16]
        DMA(bs_tok, a_bs[l].rearrange("g t -> t g"), [], ["bs_tok"], "k_bs", allow_slow_non_contiguous=True)
        for g in range(8):
            TT("dve", ws3[:, g, :], ws3[:, g, :], tril01, ALU.mult, ["ws_tok", "cst_f"], ["ws_tok"])
        OP("dve", "tensor_reduce", ["ws_tok"], ["rw_tok"], out=rw_tok, in_=ws3, op=ALU.add, axis=AX.X)
        TS("dve", rw_tok, rw_tok, 0.5, None, ALU.mult, None, ["rw_tok"], ["rw_tok"])
        TS("dve", bs_tok, bs_tok, 0.5, None, ALU.mult, None, ["bs_tok"], ["bs_tok"])
        for g in range(8):
            TS("dve", biasH[:, g * P:(g + 1) * P], lnbB[:, g * P:(g + 1) * P], rw_tok[:, g:g + 1],
               bs_tok[:, g:g + 1], ALU.mult, ALU.add, ["lnbB", "rw_tok", "bs_tok"], ["biasH"])
        ws_b = ar.alloc(8 * P, BF16)
        COPY("dve", ws_b, ws_tok, ["ws_tok"], ["ws_b"])
        for g in range(8):
            TR(psb16(0)[:, g * P:(g + 1) * P], ws_b[:, g * P:(g + 1) * P], ident, ["ws_b", "ident"], [("ps", 0)])
        COPY("dve", WsT, psb16(0), [("ps", 0)], ["WsT"])
        DMA(sgcol, subln_g[l, :].rearrange("(p o) -> p o", o=1), [], ["sgcol"], "k_sg")
        TS("dve", sgcol, sgcol, (1.0 - li) * 0.5, None, ALU.mult, None, ["sgcol"], ["sgcol"])
        lq = ar.alloc(P, F32)
        lk = ar.alloc(P, F32)
        DMA(lq, lam_q[l, :].partition_broadcast(P), [], ["lq"], "k_lq")
        DMA(lk, lam_k[l, :].partition_broadcast(P), [], ["lk"], "k_lk")
        TT("dve", lq, lq, lk, ALU.mult, ["lq", "lk"], ["lq"])
        lsum = small[:, 16:18]
        OP("dve", "tensor_reduce", ["lq"], ["lsum"], out=lsum, in_=lq.rearrange("p (m d) -> p m d", m=2),
           op=ALU.add, axis=AX.X)
        lexp = small[:, 18:20]
        ACT(lexp, lsum, AF.Exp, ["lsum"], ["lexp"])
        TT("dve", neglam, lexp[:, 1:2], lexp[:, 0:1], ALU.subtract, ["lexp"], ["neglam"])
        TS("dve", neglam, neglam, -li, None, ALU.add, None, ["neglam"], ["neglam"])
        ar.reset(pm)

        T.barrier()
        ar.reset(persist_mark)
        NTB = TBA // P
        NHH = TBA // 512
        NWS = 8
        wslots = [ar.alloc(8 * 512, BF16) for _ in range(NWS)]
        hT = ar.alloc(8 * TBA, BF16)
        hT3 = hT.rearrange("p (c t) -> p c t", c=8)
        aT = ar.alloc(8 * TBA, BF16)
        aT3 = aT.rearrange("p (c t) -> p c t", c=8)
        xt = [ar.alloc(D, F32) for _ in range(2)]
        junk = ar.alloc(D, BF16)
        xn = [ar.alloc(D, BF16) for _ in range(2)]
        ssA = ar.alloc(4, F32)
        gu = [ar.alloc(D, BF16) for _ in range(2)]
        thz = [ar.alloc(D, BF16) for _ in range(2)]
        s2z = [ar.alloc(D, BF16) for _ in range(2)]
        gv = [ar.alloc(D, F32) for _ in range(2)]
        vhat = [ar.alloc(D, BF16) for _ in range(2)]
        t1 = [ar.alloc(D, F32) for _ in range(2)]
        a_tok = [ar.alloc(D, BF16) for _ in range(2)]
        stats = ar.alloc(32, F32)
        stg = [ar.alloc(512, BF16) for _ in range(4)]
        tht = [ar.alloc(512, BF16) for _ in range(2)]
        vst = [ar.alloc(D, BF16) for _ in range(2)]

        wstate = {"n": 0}

        def load_w(src3, col0):
            s = wstate["n"] % NWS
            wstate["n"] += 1
            w3 = wslots[s].rearrange("p (c n) -> p c n", c=8)
            DMA(w3, src3[:, :, col0:col0 + 512], [("wb", l)], [("wsl", s)], ("wsl", s))
            return s, w3

        win3 = wb_in[l].rearrange("(c p) n -> p c n", p=P)
        wa3 = wb_a[l].rearrange("(c p) n -> p c n", p=P)

        pj = {"n": 0}

        def pj_bank():
            b = 2 + (pj["n"] % 6)
            pj["n"] += 1
            return b

        def pj_pair():
            if pj["n"] % 2 == 1:
                pj["n"] += 1
            b = 2 + (pj["n"] % 6)
            pj["n"] += 2
            return b

        trn = {"n": 0}

        def tr_bank():
            b = trn["n"] % 2
            trn["n"] += 1
            return b

        cnt = {"x": 0, "tok": 0, "stg": 0, "tht": 0, "vst": 0}

        ssAr = [ar.alloc(4, F32) for _ in range(3)]
        statr = [ar.alloc(20, F32) for _ in range(2)]
        xt3 = xt + [ar.alloc(D, F32)]

        def hres(hh):
            return [("hT", 4 * hh + t) for t in range(4)]

        prefetched = {}
        for blk in range(S // TBA):
            t0 = blk * TBA
            W = {}
            W["wu"] = prefetched.pop("wu") if "wu" in prefetched else [load_w(win3, C_U + i * 512) for i in range(2)]
            W["wv"] = prefetched.pop("wv") if "wv" in prefetched else [load_w(win3, C_VA + i * 512) for i in range(2)]

            xslot = {}

            def T1a(j, t0=t0):
                xi = cnt["x"] % 3
                ni = cnt["x"] % 2
                cnt["x"] += 1
                xslot[j] = ni
                xtile = xt3[xi]
                sA = ssAr[xi]
                DMA(xtile, x_src[t0 + j * P:t0 + (j + 1) * P, :], [], [("xt", xi)], ("xt", xi))
                ACT(junk, xtile, AF.Square, [("xt", xi)], ["junk", ("ssA", xi)], accum=sA[:, 0:1])
                rsqrt_pool(sA[:, 1:2], sA[:, 0:1], 1.0 / D, EPS, sA[:, 2:3], [("ssA", xi)], [("rstdA", xi)],
                           ("tmpA", xi))
                STT(xn[ni], xtile, sA[:, 1:2], gB, ALU.mult, ALU.mult, [("xt", xi), ("rstdA", xi), "gB"],
                    [("xn", ni)])

            def T1b(j):
                ni = xslot[j]
                tb = tr_bank()
                for c in range(8):
                    TR(psb16(tb)[:, c * P:(c + 1) * P], xn[ni][:, c * P:(c + 1) * P], ident,
                       [("xn", ni), "ident"], [("ps", tb)])
                COPY("act", hT3[:, :, j * P:(j + 1) * P], psb16(tb).rearrange("p (c t) -> p c t", c=8),
                     [("ps", tb)], [("hT", j)])

            def tokproj(wpair, j):
                tsl = slice(j * P, (j + 1) * P)
                b = pj_pair()
                for hh in range(2):
                    sw, w3 = wpair[hh]
                    for k in range(8):
                        MM(ps[:, b + hh, :], hT3[:, k, tsl], w3[:, k, :], k == 0, k == 7,
                           [("hT", j), ("wsl", sw)], [("ps", b + hh)])
                return b

            def T2(j):
                ti = j % 2
                stt = statr[ti]
                bu = tokproj(W["wu"], j)
                ACT(gu[ti], psb(bu, 2), AF.Gelu, [("ps", bu), ("ps", bu + 1)], [("gu", ti)])
                bv = tokproj(W["wv"], j)
                ACT(gv[ti], psb(bv, 2), AF.Gelu, [("ps", bv), ("ps", bv + 1)], [("gv", ti)])
                for hh in range(2):
                    OP("dve", "bn_stats", [("gv", ti)], [("st6", ti)], out=stt[:, hh * 6:(hh + 1) * 6],
                       in_=gv[ti][:, hh * 512:(hh + 1) * 512])
                OP("dve", "bn_aggr", [("st6", ti)], [("mv", ti)], out=stt[:, 12:14], in_=stt[:, 0:12])
                rsqrt_pool(stt[:, 14:15], stt[:, 13:14], 1.0, EPS, stt[:, 15:16], [("mv", ti)], [("rstdV", ti)],
                           ("tmpV", ti))
                bz = tokproj(W["wz"], j)
                ACT(thz[ti], psb(bz, 2), AF.Tanh, [("ps", bz), ("ps", bz + 1)], [("thz", ti)], scale=0.5)
                STT(stt[:, 16:17], stt[:, 12:13], -1.0, stt[:, 14:15], ALU.mult, ALU.mult,
                    [("mv", ti), ("rstdV", ti)], [("nmr", ti)])
                TS("dve", vhat[ti], gv[ti], stt[:, 14:15], stt[:, 16:17], ALU.mult, ALU.add,
                   [("gv", ti), ("rstdV", ti), ("nmr", ti)], [("vhat", ti)])
                STT(s2z[ti], thz[ti], 1.0, psb(bz, 2), ALU.add, ALU.mult,
                    [("thz", ti), ("ps", bz), ("ps", bz + 1)], [("s2z", ti)])
                TT("dve", gu[ti], gu[ti], s2z[ti], ALU.mult, [("gu", ti), ("s2z", ti)], [("gu", ti)])

            def T5(j, t0=t0):
                b = tokproj(W["wvb"], j)
                vi = cnt["vst"] % 2
                cnt["vst"] += 1
                COPY("act", vst[vi], psb(b, 2), [("ps", b), ("ps", b + 1)], [("vst", vi)])
                DMA(VS[t0 + j * P:t0 + (j + 1) * P, :], vst[vi], [("vst", vi)], [], ("vst", vi))

            def T3(j):
                ti = j % 2
                by = pj_pair()
                for g in range(8):
                    MM(ps[:, by + g // 4, (g % 4) * P:(g % 4 + 1) * P], WsT[:, g * P:(g + 1) * P],
                       vhat[ti][:, g * P:(g + 1) * P], True, True, ["WsT", ("vhat", ti)], [("ps", by + g // 4)])
                TT("dve", t1[ti], psb(by, 2), lngB, ALU.mult, [("ps", by), ("ps", by + 1), "lngB"], [("t1", ti)])
                TT("pool", t1[ti], t1[ti], biasH, ALU.add, [("t1", ti), "biasH"], [("t1", ti)])
                TT("dve", a_tok[ti], t1[ti], gu[ti], ALU.mult, [("t1", ti), ("gu", ti)], [("a_tok", ti)])

            def T4(j):
                ti = j % 2
                tb = tr_bank()
                for c in range(8):
                    TR(psb16(tb)[:, c * P:(c + 1) * P], a_tok[ti][:, c * P:(c + 1) * P], ident,
                       [("a_tok", ti), "ident"], [("ps", tb)])
                COPY("act", aT3[:, :, j * P:(j + 1) * P], psb16(tb).rearrange("p (c t) -> p c t", c=8),
                     [("ps", tb)], [("aT", j)])

            seq = [("q", C_Q), ("k", C_K), ("zb", C_ZB), ("gb", C_GB)]
            order = [(nm, c0, i) for nm, c0 in seq for i in range(2)]
            pending = {}

            def s4_load(oi):
                if oi < len(order):
                    nm, c0, i = order[oi]
                    if (nm, i) not in pending:
                        pending[(nm, i)] = load_w(win3, c0 + i * 512)

            def s4_item(oi, t0=t0):
                nm, c0, i = order[oi]
                s4_load(oi + 1)
                s4_load(oi + 2)
                sw, w3 = pending.pop((nm, i))
                dst = {"q": QT, "k": KT, "zb": ZBs, "gb": THB}[nm]
                for ctl in range(4):
                    ct = i * 4 + ctl
                    for hh in range(NHH):
                        b = pj_bank()
                        for k in range(8):
                            MM(ps[:, b, :], w3[:, k, ctl * P:(ctl + 1) * P], hT3[:, k, hh * 512:(hh + 1) * 512],
                               k == 0, k == 7, hres(hh) + [("wsl", sw)], [("ps", b)])
                        si = cnt["stg"] % 4
                        cnt["stg"] += 1
                        if nm in ("q", "k"):
                            COPY("act", stg[si], ps[:, b, :], [("ps", b)], [("stg", si)])
                        elif nm == "gb":
                            ACT(stg[si], ps[:, b, :], AF.Tanh, [("ps", b)], [("stg", si)], scale=0.5)
                        else:
                            hi = cnt["tht"] % 2
                            cnt["tht"] += 1
                            ACT(tht[hi], ps[:, b, :], AF.Tanh, [("ps", b)], [("tht", hi)], scale=0.5)
                            STT(stg[si], tht[hi], 1.0, ps[:, b, :], ALU.add, ALU.mult,
                                [("tht", hi), ("ps", b)], [("stg", si)])
                        DMA(dst[ct, :, t0 + hh * 512:t0 + (hh + 1) * 512], stg[si], [("stg", si)], [],
                            ("stg", si), q="pool")

            T1a(0)
            T1b(0)
            for it in range(1, NTB + 1):
                if it < NTB:
                    T1a(it)
                if it == 1:
                    W["wz"] = [load_w(win3, C_ZA + i * 512) for i in range(2)]
                    W["wvb"] = [load_w(win3, C_VB + i * 512) for i in range(2)]
                T2(it - 1)
                if it < NTB:
                    T1b(it)
                T5(it - 1)
                if it - 2 >= 0:
                    T3(it - 2)
                if it - 3 >= 0:
                    T4(it - 3)
            s4_load(0)
            s4_load(1)
            T3(NTB - 1)
            T4(NTB - 2)
            s4_item(0)
            T4(NTB - 1)
            for oi in range(1, len(order)):
                s4_item(oi)
            s5w = [(load_w(win3, C_GA + i * 512), load_w(wa3, i * 512)) for i in range(2)]
            if blk + 1 < S // TBA:
                prefetched["wu"] = [load_w(win3, C_U + i * 512) for i in range(2)]
                prefetched["wv"] = [load_w(win3, C_VA + i * 512) for i in range(2)]
            for i in range(2):
                (sg, wg3), (sa, wa_3) = s5w[i]
                for ctl in range(4):
                    ct = i * 4 + ctl
                    for hh in range(NHH):
                        bg = pj_bank()
                        for k in range(8):
                            MM(ps[:, bg, :], wg3[:, k, ctl * P:(ctl + 1) * P], hT3[:, k, hh * 512:(hh + 1) * 512],
                               k == 0, k == 7, hres(hh) + [("wsl", sg)], [("ps", bg)])
                        hi = cnt["tht"] % 2
                        cnt["tht"] += 1
                        ACT(tht[hi], ps[:, bg, :], AF.Tanh, [("ps", bg)], [("tht", hi)], scale=0.5)
                        by = pj_bank()
                        for k in range(8):
                            MM(ps[:, by, :], wa_3[:, k, ctl * P:(ctl + 1) * P], aT3[:, k, hh * 512:(hh + 1) * 512],
                               k == 0, k == 7, [("aT", 4 * hh + t) for t in range(4)] + [("wsl", sa)], [("ps", by)])
                        si = cnt["stg"] % 4
                        cnt["stg"] += 1
                        STT(stg[si], tht[hi], 1.0, ps[:, by, :], ALU.add, ALU.mult,
                            [("tht", hi), ("ps", by)], [("stg", si)])
                        DMA(GAYA[ct, :, t0 + hh * 512:t0 + (hh + 1) * 512], stg[si], [("stg", si)], [],
                            ("stg", si), q="pool")

        T.barrier()
        ar.reset(persist_mark)
        wbS = ar.alloc(8 * D, BF16)
        woS = ar.alloc(8 * D, BF16)
        wpgS = ar.alloc(8 * D, BF16)
        wplS = ar.alloc(2 * D, BF16)
        wbS3 = wbS.rearrange("p (c n) -> p c n", c=8)
        woS3 = woS.rearrange("p (c n) -> p c n", c=8)
        wpgS3 = wpgS.rearrange("p (c n) -> p c n", c=8)
        wplS3 = wplS.rearrange("p (c n) -> p c n", c=2)
        cw_mark = ar.mark()
        HB = []
        for i in range(2):
            HB.append({
                "kt": ar.alloc(S, BF16), "qa": ar.alloc(S, BF16), "qb": ar.alloc(S, BF16),
                "v": ar.alloc(NT * 130, BF16), "zb": ar.alloc(S, BF16),
            })
        QB = 256
        NQB2 = S // QB
        DEFER = 12
        pT = [ar.alloc(2 * QB, BF16) for _ in range(3)]
        o_f = [ar.alloc(P, F32) for _ in range(6)]
        o_junk = ar.alloc(P, F32)
        on_b = [ar.alloc(P, BF16) for _ in range(20)]
        ost = [ar.alloc(QB, BF16) for _ in range(10)]
        rsB = [ar.alloc(8, F32) for _ in range(6)]
        oS = [ar.alloc(260, F32) for _ in range(6)]
        for i in range(2):
            MEMSET("dve", HB[i]["qa"][64:128, :], 0.0, [("qaz", i)])
            MEMSET("dve", HB[i]["qb"][0:64, :], 0.0, [("qbz", i)])
            v3 = HB[i]["v"].rearrange("p (k e) -> p k e", e=130)
            MEMSET("dve", v3[:, :, 128:130], 1.0, [("v", i)])

        def load_head(h):
            i = h % 2
            hb = HB[i]
            DMA(hb["kt"], KT[h], [], [("kt", i)], ("hk", i))
            DMA(hb["qa"][0:64, :], QT[h, 0:64, :], [], [("qa", i)], ("hqa", i))
            DMA(hb["qb"][64:128, :], QT[h, 64:128, :], [], [("qb", i)], ("hqb", i))
            v3 = hb["v"].rearrange("p (k e) -> p k e", e=130)
            DMA(v3[:, :, 0:128], VS.rearrange("(k p) c -> p k c", p=P)[:, :, h * P:(h + 1) * P],
                [], [("v", i)], ("hv", i))
            DMA(hb["zb"], ZBs[h], [], [("zb", i)], ("hz", i))

        def obank(g, qi):
            return 3 + 2 * (g % 2) + qi

        def oacc(g, m, qi):
            return ps[:, obank(g, qi), m * 130:m * 130 + 129]

        qorder = []
        lo, hi_ = 0, NQB2 - 1
        while lo <= hi_:
            qorder.append(hi_)
            if lo != hi_:
                qorder.append(lo)
            lo += 1
            hi_ -= 1
        steps = [(h, Qb, kt) for h in range(NHB) for Qb in qorder for kt in range(2 * Qb + 2)]
        NS = len(steps)
        gof = {}
        for (h_, Qb_, kt_) in steps:
            gof.setdefault((h_, Qb_), len(gof))
        head_first = {}
        for n, (h, Qb, kt) in enumerate(steps):
            head_first.setdefault(h, n)

        def emit_S(n):
            h, Qb, kt = steps[n]
            i = h % 2
            hb = HB[i]
            jd = kt - 2 * Qb
            c0 = max(jd, 0) * P
            sl = n % 3
            qsl = slice(Qb * QB + c0, Qb * QB + QB)
            for m in range(2):
                qsrc = hb["qa"] if m == 0 else hb["qb"]
                qres = ("qa", i) if m == 0 else ("qb", i)
                zres = ("qaz", i) if m == 0 else ("qbz", i)
                if jd >= 0:
                    MM(ps[:, sl, m * QB + c0:m * QB + c0 + P], ident, maskneg, m == 0, False,
                       ["ident", "maskneg"], [("S", sl)])
                    MM(ps[:, sl, m * QB + c0:(m + 1) * QB], hb["kt"][:, kt * P:(kt + 1) * P], qsrc[:, qsl], False, True,
                       [("kt", i), qres, zres], [("S", sl)])
                else:
                    MM(ps[:, sl, m * QB:(m + 1) * QB], hb["kt"][:, kt * P:(kt + 1) * P], qsrc[:, qsl], m == 0, True,
                       [("kt", i), qres, zres], [("S", sl)])

        def emit_EXP(n):
            h, Qb, kt = steps[n]
            c0 = max(kt - 2 * Qb, 0) * P
            sl = n % 3
            pi = n % 3
            p3 = pT[pi].rearrange("p (m q) -> p m q", m=2)
            s3 = ps[:, sl, :].rearrange("p (m q) -> p m q", m=2)
            ACT(p3[:, :, c0:QB], s3[:, :, c0:QB], AF.Exp, [("S", sl)], [("pT", pi)], scale=0.125)

        def emit_PV(n):
            h, Qb, kt = steps[n]
            i = h % 2
            hb = HB[i]
            g = gof[(h, Qb)]
            v3 = hb["v"].rearrange("p (k e) -> p k e", e=130)
            jd = kt - 2 * Qb
            pi = n % 3
            p3 = pT[pi].rearrange("p (m q) -> p m q", m=2)
            for qi in range(max(jd, 0), 2):
                if kt == 0:
                    MM(ps[:, obank(g, qi), :], zeros, hb["kt"][:, 0:512], True, True, ["zeros", ("kt", i)],
                       [("Ob", obank(g, qi))])
                for m in range(2):
                    MM(oacc(g, m, qi), p3[:, m, qi * P:(qi + 1) * P], v3[:, kt, 0:129], False, kt == 2 * Qb + qi,
                       [("pT", pi), ("v", i)], [("Ob", obank(g, qi))])

        evs = {"n": 0}
        deferred = []

        def run_deferred(upto):
            while deferred and deferred[0][0] <= upto:
                deferred.pop(0)[1]()

        def defer(due, fn, head):
            k = len(deferred)
            while k > 0 and deferred[k - 1][0] > due:
                k -= 1
            deferred.insert(k, (due, fn, head))

        fin_due = {"last": -100}

        def emit_evac(n):
            h, Qb, kt = steps[n]
            i = h % 2
            hb = HB[i]
            g = gof[(h, Qb)]
            qi = kt - 2 * Qb
            ob = obank(g, qi)
            oi = g % 10
            en = evs["n"]
            evs["n"] += 1
            ei = en % 6
            bi = en % 20
            rs = rsB[ei]
            ki = en % 6
            osb = oS[ki]
            COPY("dve", osb[:, 0:259], ps[:, ob, 0:259], [("Ob", ob)], [("osb", ki)])
            OP("dve", "reciprocal", [("osb", ki)], [("rs", ei)], out=rs[:, 0:2], in_=osb[:, 128:259:130])
            TT("dve", rs[:, 2:3], rs[:, 1:2], neglam, ALU.mult, [("rs", ei), "neglam"], [("rs2", ei)])
            TS("dve", o_f[ei], osb[:, 0:P], rs[:, 0:1], None, ALU.mult, None,
               [("osb", ki), ("rs", ei)], [("o_f", ei)])
            STT(o_f[ei], osb[:, 130:130 + P], rs[:, 2:3], o_f[ei], ALU.mult, ALU.add,
                [("osb", ki), ("rs2", ei), ("o_f", ei)], [("o_f", ei)])
            OP("dve", "scalar_tensor_tensor", [("o_f", ei)], ["o_junk", ("ssq", ei)], out=o_junk,
               in0=o_f[ei], scalar=1.0, in1=o_f[ei], op0=ALU.mult, op1=ALU.mult, accum_out=rs[:, 3:4])
            rsqrt_pool(rs[:, 4:5], rs[:, 3:4], 1.0 / P, EPS, rs[:, 5:6], [("ssq", ei)], [("rstdO", ei)],
                       ("tmpO", ei))

            def part2():
                TS("dve", on_b[bi], o_f[ei], rs[:, 4:5], None, ALU.mult, None,
                   [("o_f", ei), ("rstdO", ei)], [("on_b", bi)])

            def finish():
                tro = psb16(7)[:, (en % 8) * P:(en % 8 + 1) * P]
                TR(tro, on_b[bi], ident, [("on_b", bi), "ident"], [("ps", 7)])
                STT(ost[oi][:, qi * P:(qi + 1) * P], tro, sgcol[:, 0:1],
                    hb["zb"][:, Qb * QB + qi * P:Qb * QB + (qi + 1) * P], ALU.mult, ALU.mult,
                    [("ps", 7), "sgcol", ("zb", i)], [("ost", oi)])
                if qi == 1:
                    DMA(OBT[h, :, Qb * QB:(Qb + 1) * QB], ost[oi], [("ost", oi)], [], ("ost", oi))

            defer(n + 4, part2, h)
            due = max(n + DEFER, fin_due["last"] + 7)
            fin_due["last"] = due
            defer(due, finish, h)

        load_head(0)
        loaded = {0}
        if l + 1 < NL:
            issue_casts(l + 1)
        emit_S(0)
        emit_S(1)
        for n in range(NS):
            h, Qb, kt = steps[n]
            if n + 2 < NS:
                emit_S(n + 2)
            emit_EXP(n)
            emit_PV(n)
            run_deferred(n)
            if h + 1 < NHB and (h + 1) not in loaded and n >= head_first[h] + 2 \
                    and all(d[2] >= h for d in deferred):
                load_head(h + 1)
                loaded.add(h + 1)
            if h + 1 < NHB and (h + 1) not in loaded and n >= head_first[h + 1] - 6:
                run_deferred(NS * 10)
                load_head(h + 1)
                loaded.add(h + 1)
            if n == min(40, NS - 1):
                for c2 in range(2):
                    DMA(wbS3[:, :, c2 * 512:(c2 + 1) * 512],
                        wb_b[l].rearrange("(c p) n -> p c n", p=P)[:, :, c2 * 512:(c2 + 1) * 512],
                        [], ["wbS"], "cw0")
                for c2 in range(2):
                    DMA(woS3[:, :, c2 * 512:(c2 + 1) * 512],
                        wb_o[l].rearrange("(c p) n -> p c n", p=P)[:, :, c2 * 512:(c2 + 1) * 512],
                        [], ["woS"], "cw1")
                for c2 in range(2):
                    DMA(wpgS3[:, :, c2 * 512:(c2 + 1) * 512],
                        wb_pg[l].rearrange("(c p) n -> p c n", p=P)[:, :, c2 * 512:(c2 + 1) * 512],
                        [], ["wpgS"], "cw2")
                DMA(wplS3, wb_ple[l].rearrange("(c p) n -> p c n", p=P), [], ["wplS"], "cw3")
            if kt - 2 * Qb >= 0:
                emit_evac(n)
        run_deferred(NS * 10)

        T.barrier()
        ar.reset(persist_mark)
        ar.reset(cw_mark)
        obB = [ar.alloc(8 * 512, BF16) for _ in range(2)]
        thB = [ar.alloc(8 * 512, BF16) for _ in range(2)]
        gyB = [ar.alloc(8 * 512, BF16) for _ in range(2)]
        mgT = ar.alloc(8 * 512, BF16)
        mgT3 = mgT.rearrange("p (c t) -> p c t", c=8)
        m1 = [ar.alloc(512, BF16) for _ in range(2)]
        xc = [ar.alloc(D, F32) for _ in range(2)]
        x1 = [ar.alloc(D, F32) for _ in range(2)]
        junkC = ar.alloc(D, BF16)
        xn1 = [ar.alloc(D, BF16) for _ in range(2)]
        hpT = [ar.alloc(8 * P, BF16) for _ in range(2)]
        thp = [ar.alloc(D, BF16) for _ in range(2)]
        pt_f = [ar.alloc(PLE, F32) for _ in range(2)]
        pt_b = [ar.alloc(PLE, BF16) for _ in range(2)]
        pTt = [ar.alloc(2 * P, BF16) for _ in range(2)]
        tC = [ar.alloc(D, F32) for _ in range(2)]
        x2 = [ar.alloc(D, F32) for _ in range(1)]
        yo = [ar.alloc(D, F32) for _ in range(2)]
        ssC = [ar.alloc(8, F32) for _ in range(2)]

        pjc = {"n": 0}

        def pjc_pair():
            b = 2 + 2 * (pjc["n"] % 3)
            pjc["n"] += 1
            return b

        trc = {"n": 0}

        def trc_bank():
            b = trc["n"] % 2
            trc["n"] += 1
            return b

        mgT_b = ar.alloc(8 * 512, BF16)
        mgTs = [mgT3, mgT_b.rearrange("p (c t) -> p c t", c=8)]
        x1r = x1 + [x2[0], ar.alloc(D, F32)]
        ssCr = ssC + [ar.alloc(8, F32), ar.alloc(8, F32)]
        obt3 = OBT.rearrange("c p t -> p c t")
        thb3 = THB.rearrange("c p t -> p c t")
        gya3 = GAYA.rearrange("c p t -> p c t")
        NBLK = S // 512

        def YB(blk):
            t0 = blk * 512
            bi = blk % 2
            ob3 = obB[bi].rearrange("p (c t) -> p c t", c=8)
            th3 = thB[bi].rearrange("p (c t) -> p c t", c=8)
            gy3 = gyB[bi].rearrange("p (c t) -> p c t", c=8)
            DMA(ob3, obt3[:, :, t0:t0 + 512], [], [("obB", bi)], ("obB", bi))
            DMA(th3, thb3[:, :, t0:t0 + 512], [], [("thB", bi)], ("thB", bi))
            DMA(gy3, gya3[:, :, t0:t0 + 512], [], [("gyB", bi)], ("gyB", bi))
            for ct in range(8):
                b = pjc_pair()
                for k in range(8):
                    MM(ps[:, b, :], wbS3[:, k, ct * P:(ct + 1) * P], ob3[:, k, :], k == 0, k == 7,
                       ["wbS", ("obB", bi)], [("ps", b)])
                mi = ct % 2
                STT(m1[mi], th3[:, ct, :], 1.0, ps[:, b, :], ALU.add, ALU.mult, [("thB", bi), ("ps", b)], [("m1", mi)])
                TT("dve", mgTs[bi][:, ct, :], m1[mi], gy3[:, ct, :], ALU.add, [("m1", mi), ("gyB", bi)],
                   [("mgT", bi)])

        def CA(g):
            blk, jt = divmod(g, 4)
            bi = blk % 2
            ti = g % 2
            xi = g % 4
            tok0 = g * P
            DMA(xc[ti], x_src[tok0:tok0 + P, :], [], [("xc", ti)], ("xc", ti))
            DMA(pt_f[ti], p_in[l, tok0:tok0 + P, :], [], [("pt_f", ti)], ("ptf", ti))
            b = pjc_pair()
            for hh in range(2):
                for k in range(8):
                    MM(ps[:, b + hh, :], mgTs[bi][:, k, jt * P:(jt + 1) * P], woS3[:, k, hh * 512:(hh + 1) * 512],
                       k == 0, k == 7, [("mgT", bi), "woS"], [("ps", b + hh)])
            STT(x1r[xi], psb(b, 2), 0.5, xc[ti], ALU.mult, ALU.add, [("ps", b), ("ps", b + 1), ("xc", ti)],
                [("x1", xi)])
            sc = ssCr[xi]
            ACT(junkC, x1r[xi], AF.Square, [("x1", xi)], ["junkC", ("ssc", xi)], accum=sc[:, 0:1])
            rsqrt_pool(sc[:, 1:2], sc[:, 0:1], 1.0 / D, EPS, sc[:, 2:3], [("ssc", xi)], [("rstdC", xi)],
                       ("tmpC", xi))
            STT(xn1[ti], x1r[xi], sc[:, 1:2], plgB, ALU.mult, ALU.mult, [("x1", xi), ("rstdC", xi), "plgB"],
                [("xn1", ti)])
            COPY("dve", pt_b[ti], pt_f[ti], [("pt_f", ti)], [("pt_b", ti)])

        def CB1(g):
            ti = g % 2
            tb = trc_bank()
            for c in range(8):
                TR(psb16(tb)[:, c * P:(c + 1) * P], xn1[ti][:, c * P:(c + 1) * P], ident,
                   [("xn1", ti), "ident"], [("ps", tb)])
            COPY("act", hpT[ti], psb16(tb), [("ps", tb)], [("hpT", ti)])
            tb = trc_bank()
            for c in range(2):
                TR(psb16(tb)[:, c * P:(c + 1) * P], pt_b[ti][:, c * P:(c + 1) * P], ident,
                   [("pt_b", ti), "ident"], [("ps", tb)])
            COPY("act", pTt[ti], psb16(tb)[:, 0:2 * P], [("ps", tb)], [("pTt", ti)])

        def CB2(g):
            ti = g % 2
            xi = g % 4
            tok0 = g * P
            sc = ssCr[xi]
            hp3 = hpT[ti].rearrange("p (c t) -> p c t", c=8)
            b = pjc_pair()
            for hh in range(2):
                for k in range(8):
                    MM(ps[:, b + hh, :], hp3[:, k, :], wpgS3[:, k, hh * 512:(hh + 1) * 512], k == 0, k == 7,
                       [("hpT", ti), "wpgS"], [("ps", b + hh)])
            ACT(thp[ti], psb(b, 2), AF.Tanh, [("ps", b), ("ps", b + 1)], [("thp", ti)], scale=0.5)
            pT3 = pTt[ti].rearrange("p (c t) -> p c t", c=2)
            b = pjc_pair()
            for hh in range(2):
                for k in range(2):
                    MM(ps[:, b + hh, :], pT3[:, k, :], wplS3[:, k, hh * 512:(hh + 1) * 512], k == 0, k == 1,
                       [("pTt", ti), "wplS"], [("ps", b + hh)])
            STT(tC[ti], thp[ti], 1.0, psb(b, 2), ALU.add, ALU.mult, [("thp", ti), ("ps", b), ("ps", b + 1)],
                [("tC", ti)])
            STT(x1r[xi], tC[ti], 0.5, x1r[xi], ALU.mult, ALU.add, [("tC", ti), ("x1", xi)], [("x1", xi)])
            if not is_last:
                DMA(xs[tok0:tok0 + P, :], x1r[xi], [("x1", xi)], [], ("x2", xi), q="pool")
            elif not last_final:
                DMA(out[tok0:tok0 + P, :], x1r[xi], [("x1", xi)], [], ("x2", xi), q="pool")
            else:
                ACT(junkC, x1r[xi], AF.Square, [("x1", xi)], ["junkC", ("ssf", xi)], accum=sc[:, 3:4])
                rsqrt_pool(sc[:, 4:5], sc[:, 3:4], 1.0 / D, EPS, sc[:, 5:6], [("ssf", xi)], [("rstdF", xi)],
                           ("tmpF", xi))
                STT(yo[ti], x1r[xi], sc[:, 4:5], fgB, ALU.mult, ALU.mult, [("x1", xi), ("rstdF", xi), "fgB"],
                    [("yo", ti)])
                DMA(out[tok0:tok0 + P, :], yo[ti], [("yo", ti)], [], ("yo", ti), q="pool")

        YB(0)
        for it in range(NT + 2):
            if it < NT:
                CA(it)
                if it % 4 == 1 and it // 4 + 1 < NBLK:
                    YB(it // 4 + 1)
            if 0 <= it - 1 < NT:
                CB1(it - 1)
            if 0 <= it - 2 < NT:
                CB2(it - 2)

    T.barrier()
    T.finalize()
    return nc, T


def make_consts():
    c = np.zeros((P, 4 * P), np.float32)
    c[:, 0:P] = np.eye(P, dtype=np.float32)
    k = np.arange(P)[:, None]
    q = np.arange(P)[None, :]
    c[:, P:2 * P] = np.where(q < k, -30000.0, 0.0)
    c[:, 2 * P:3 * P] = np.where(q <= k, 1.0, 0.0)
    c[:, 3 * P:4 * P] = 1.0
    return c


_PROG_CACHE = {}


def _get_program(S, NL):
    key = (S, NL)
    if key not in _PROG_CACHE:
        _PROG_CACHE[key] = build_program(S, NL)
    return _PROG_CACHE[key][0]


def make_in_maps(x, p, norm_g, w_in, a_ln_g, a_ln_b, a_ws, a_bs, lam_q, lam_k, subln_g,
                 w_a_out, w_b_out, w_o, ple_norm_g, w_ple, w_ple_gate, final_g, n_cores):
    f = lambda a: np.ascontiguousarray(np.asarray(a, dtype=np.float32))
    NL = w_in.shape[0]
    shared = {
        "norm_g": f(norm_g), "w_in": f(w_in), "a_ln_g": f(a_ln_g), "a_ln_b": f(a_ln_b),
        "a_ws": f(a_ws), "a_bs": f(a_bs), "lam_q": f(lam_q).reshape(NL, P), "lam_k": f(lam_k).reshape(NL, P),
        "subln_g": f(subln_g), "w_a_out": f(w_a_out), "w_b_out": f(w_b_out), "w_o": f(w_o),
        "ple_norm_g": f(ple_norm_g), "w_ple": f(w_ple), "w_ple_gate": f(w_ple_gate),
        "final_g": f(final_g).reshape(1, D), "consts": make_consts(),
    }
    x = f(x)
    p = f(p)
    maps = []
    for c in range(n_cores):
        m = dict(shared)
        m["x"] = np.ascontiguousarray(x[c])
        m["p"] = np.ascontiguousarray(p[:, c])
        maps.append(m)
    return maps


def kernel(x, p, norm_g, w_in, a_ln_g, a_ln_b, a_ws, a_bs, lam_q, lam_k, subln_g,
           w_a_out, w_b_out, w_o, ple_norm_g, w_ple, w_ple_gate, final_g):
    x = np.asarray(x)
    B, S, _ = x.shape
    NL = np.asarray(w_in).shape[0]
    assert B == N_CORES
    nc = _get_program(S, NL)
    maps = make_in_maps(x, p, norm_g, w_in, a_ln_g, a_ln_b, a_ws, a_bs, lam_q, lam_k, subln_g,
                        w_a_out, w_b_out, w_o, ple_norm_g, w_ple, w_ple_gate, final_g, N_CORES)
    res = run_bass_kernel_spmd(nc, maps, core_ids=list(range(N_CORES)))
    return np.stack([np.asarray(r["out"], dtype=np.float32) for r in res.results], axis=0)
```

```python
import math
import numpy as np
import concourse.bass as bass
import concourse.mybir as mybir
from concourse.bass_utils import run_bass_kernel_spmd

F32 = mybir.dt.float32
BF16 = mybir.dt.bfloat16
AF = mybir.ActivationFunctionType
ALU = mybir.AluOpType
AX = mybir.AxisListType

P = 128
D = 1024
NCH = 8
INW = 9216
PLE = 256
EPS = 1e-6
C_U, C_VA, C_ZA, C_Q, C_K, C_VB, C_ZB, C_GA, C_GB = [i * 1024 for i in range(9)]
N_CORES = 8


def lambda_init(i):
    return 0.8 - 0.6 * math.exp(-0.3 * i)


class _Op:
    __slots__ = ("eng", "fn", "reads", "writes", "dkey", "deps", "sig", "event", "barrier")

    def __init__(self, eng, fn, reads, writes, dkey):
        self.eng = eng
        self.fn = fn
        self.reads = reads
        self.writes = writes
        self.dkey = dkey
        self.deps = ()
        self.sig = False
        self.event = None
        self.barrier = False


class Tracker:
    ENGS = ("pe", "act", "dve", "pool", "sp")

    def __init__(self, nc):
        self.nc = nc
        self.ops = []
        self.eng_obj = {"pe": nc.tensor, "act": nc.scalar, "dve": nc.vector,
                        "pool": nc.gpsimd, "sp": nc.sync}

    def op(self, eng, fn, reads=(), writes=(), dkey=None):
        self.ops.append(_Op(eng, fn, tuple(reads), tuple(writes), dkey))

    def barrier(self):
        for e in self.ENGS:
            o = _Op(e, None, (), (), None)
            o.barrier = True
            self.ops.append(o)

    def finalize(self):
        ops = self.ops
        last_w = {}
        readers = {}
        last_eng = {}
        last_key = {}
        i = 0
        n = len(ops)
        while i < n:
            op = ops[i]
            if op.barrier:
                deps = set(last_eng.values()) | set(v for k, v in last_key.items()
                                                   if not (isinstance(k, tuple) and k[0] == "cast"))
                j = i
                while j < n and ops[j].barrier:
                    ops[j].deps = tuple(deps)
                    j += 1
                for d in deps:
                    if ops[d].dkey is None:
                        ops[d].sig = True
                last_w = {k: v for k, v in last_w.items() if isinstance(k, tuple) and k[0] == "wb"}
                readers = {}
                i = j
                continue
            deps = set()
            for r in op.reads:
                w = last_w.get(r)
                if w is not None:
                    deps.add(w)
            for wr in op.writes:
                w = last_w.get(wr)
                if w is not None:
                    deps.add(w)
                rd = readers.get(wr)
                if rd:
                    deps.update(rd.values())
            rk = op.dkey if op.dkey is not None else op.eng
            for r in op.reads:
                readers.setdefault(r, {})[rk] = i
            for wr in op.writes:
                last_w[wr] = i
                readers[wr] = {}
            deps.discard(i)
            if op.eng == "pe" and op.dkey is None:
                deps = {d for d in deps if not (ops[d].eng == "pe" and ops[d].dkey is None)}
            op.deps = tuple(deps)
            for d in deps:
                if ops[d].dkey is None:
                    ops[d].sig = True
            if op.dkey is not None:
                last_key[op.dkey] = i
            else:
                last_eng[op.eng] = i
            i += 1

        nc = self.nc
        keys = []
        for op in ops:
            if op.dkey is not None and op.dkey not in keys:
                keys.append(op.dkey)
        sems = {}
        for e in self.ENGS:
            sems[e] = nc.alloc_semaphore("prog_" + e)
        for k in keys:
            sems[k] = nc.alloc_semaphore("dma_" + str(len(sems)))
        cnt = {k: 0 for k in sems}
        seen = {e: {} for e in self.ENGS}
        for op in ops:
            eo = self.eng_obj[op.eng]
            need = {}
            for d in op.deps:
                ev = ops[d].event
                if ev is None:
                    raise RuntimeError("dependency on unsignaled op")
                if need.get(ev[0], 0) < ev[1]:
                    need[ev[0]] = ev[1]
            sn = seen[op.eng]
            for k, v in need.items():
                if sn.get(k, 0) < v:
                    eo.wait_ge(sems[k], v)
                    sn[k] = v
            if op.fn is None:
                continue
            ins = op.fn()
            if op.dkey is not None:
                cnt[op.dkey] += 16
                ins.then_inc(sems[op.dkey], 16)
                op.event = (op.dkey, cnt[op.dkey])
            elif op.sig:
                cnt[op.eng] += 1
                ins.then_inc(sems[op.eng], 1)
                op.event = (op.eng, cnt[op.eng])
        fin = {}
        for k in keys:
            if cnt[k] > 0:
                fin[k] = cnt[k]
        for k, v in fin.items():
            if seen["sp"].get(k, 0) < v:
                nc.sync.wait_ge(sems[k], v)
        self.n_ops = len(ops)


class Arena:
    def __init__(self, nc, nbytes):
        self.n16 = nbytes // 2
        self.t = nc.alloc_sbuf_tensor("arena", [P, self.n16], BF16)
        self.off = 0

    def mark(self):
        return self.off

    def reset(self, m):
        self.off = m

    def alloc(self, nelem, dtype):
        n16 = nelem * (2 if dtype == F32 else 1)
        n16 = (n16 + 15) // 16 * 16
        assert self.off + n16 <= self.n16, f"arena overflow {self.off + n16} > {self.n16}"
        ap = self.t[:, self.off:self.off + n16]
        self.off += n16
        if dtype == F32:
            return ap.bitcast(F32)[:, 0:nelem]
        return ap[:, 0:nelem]


def build_program(S, NL, TBA=1024, last_final=True):
    assert S % 512 == 0
    TBA = min(TBA, S)
    nc = bass.Bass("TRN2", target_bir_lowering=False)
    T = Tracker(nc)
    NT = S // P
    NQB = S // 512
    NHB = 8

    def din(name, shape, dt=F32):
        return nc.dram_tensor(name, shape, dt, kind="ExternalInput").ap()

    def dscr(name, shape, dt):
        return nc.dram_tensor(name, shape, dt, kind="Internal").ap()

    x_in = din("x", [S, D])
    p_in = din("p", [NL, S, PLE])
    norm_g = din("norm_g", [NL, D])
    w_in = din("w_in", [NL, D, INW])
    a_ln_g = din("a_ln_g", [NL, D])
    a_ln_b = din("a_ln_b", [NL, D])
    a_ws = din("a_ws", [NL, 8, P, P])
    a_bs = din("a_bs", [NL, 8, P])
    lam_q = din("lam_q", [NL, P])
    lam_k = din("lam_k", [NL, P])
    subln_g = din("subln_g", [NL, P])
    w_a_out = din("w_a_out", [NL, D, D])
    w_b_out = din("w_b_out", [NL, D, D])
    w_o = din("w_o", [NL, D, D])
    ple_norm_g = din("ple_norm_g", [NL, D])
    w_ple = din("w_ple", [NL, PLE, D])
    w_ple_gate = din("w_ple_gate", [NL, D, D])
    final_g = din("final_g", [1, D])
    consts = din("consts", [P, 4 * P])
    out = nc.dram_tensor("out", [S, D], F32, kind="ExternalOutput").ap()

    wb_in = dscr("wb_in", [NL, D, INW], BF16)
    wb_a = dscr("wb_a", [NL, D, D], BF16)
    wb_b = dscr("wb_b", [NL, D, D], BF16)
    wb_o = dscr("wb_o", [NL, D, D], BF16)
    wb_pg = dscr("wb_pg", [NL, D, D], BF16)
    wb_ple = dscr("wb_ple", [NL, PLE, D], BF16)
    xs = dscr("xs", [S, D], F32)
    QT = dscr("QT", [8, P, S], BF16)
    KT = dscr("KT", [8, P, S], BF16)
    VS = dscr("VS", [S, D], BF16)
    ZBs = dscr("ZBs", [8, P, S], BF16)
    THB = dscr("THB", [8, P, S], BF16)
    GAYA = dscr("GAYA", [8, P, S], BF16)
    OBT = dscr("OBT", [8, P, S], BF16)

    ar = Arena(nc, 206 * 1024)
    ps = nc.alloc_psum_tensor("ps", [P, 8, 512], F32)

    eng = T.eng_obj
    pe, act, dve, pool, sp = eng["pe"], eng["act"], eng["dve"], eng["pool"], eng["sp"]

    def DMA(out_ap, in_ap, reads, writes, dkey, q="sp", **kw):
        e = eng[q]
        T.op(q, lambda: e.dma_start(out=out_ap, in_=in_ap, **kw), reads, writes, dkey)

    def MM(out_ap, lhsT, rhs, start, stop, reads, writes):
        T.op("pe", lambda: pe.matmul(out_ap, lhsT=lhsT, rhs=rhs, start=start, stop=stop,
                                     skip_group_check=True), reads, writes)

    def TR(out_ap, in_ap, ident, reads, writes):
        T.op("pe", lambda: pe.transpose(out_ap, in_ap, ident), reads, writes)

    def ACT(out_ap, in_ap, func, reads, writes, scale=None, accum=None):
        kw = {}
        if scale is not None:
            kw["scale"] = scale
        if accum is not None:
            kw["accum_out"] = accum
        T.op("act", lambda: act.activation(out=out_ap, in_=in_ap, func=func, **kw), reads, writes)

    def OP(e, name, reads, writes, **kw):
        eo = eng[e]
        T.op(e, lambda: getattr(eo, name)(**kw), reads, writes)

    def TS(e, out_ap, in0, s1, s2, op0, op1, reads, writes):
        eo = eng[e]
        if op1 is None:
            T.op(e, lambda: eo.tensor_scalar(out=out_ap, in0=in0, scalar1=s1, scalar2=None, op0=op0),
                 reads, writes)
        else:
            T.op(e, lambda: eo.tensor_scalar(out=out_ap, in0=in0, scalar1=s1, scalar2=s2, op0=op0, op1=op1),
                 reads, writes)

    def STT(out_ap, in0, scalar, in1, op0, op1, reads, writes):
        T.op("dve", lambda: dve.scalar_tensor_tensor(out=out_ap, in0=in0, scalar=scalar, in1=in1,
                                                     op0=op0, op1=op1), reads, writes)

    def TT(e, out_ap, in0, in1, op, reads, writes):
        eo = eng[e]
        T.op(e, lambda: eo.tensor_tensor(out=out_ap, in0=in0, in1=in1, op=op), reads, writes)

    def COPY(e, out_ap, in_ap, reads, writes):
        if e == "act":
            ACT(out_ap, in_ap, AF.Copy, reads, writes)
        else:
            eo = eng[e]
            T.op(e, lambda: eo.tensor_copy(out=out_ap, in_=in_ap), reads, writes)

    def MEMSET(e, ap, val, writes):
        eo = eng[e]
        T.op(e, lambda: eo.memset(ap, val), (), writes)

    def rsqrt_pool(out_ap, in_ap, mul, add, tmp_ap, reads, writes, tmpres):
        TS("pool", tmp_ap, in_ap, mul, add, ALU.mult, ALU.add, reads, [tmpres])
        TT("pool", out_ap, tmp_ap, neghalf[:, 0:tmp_ap.shape[1]], ALU.pow, [tmpres, "neghalf"], writes)

    cst_f = ar.alloc(4 * P, F32)
    ident = ar.alloc(P, BF16)
    maskneg = ar.alloc(P, BF16)
    zeros = ar.alloc(P, BF16)
    tril01 = cst_f[:, 2 * P:3 * P]
    neghalf = ar.alloc(16, F32)
    fgB = ar.alloc(D, F32)
    gB = ar.alloc(D, F32)
    plgB = ar.alloc(D, F32)
    lngB = ar.alloc(D, F32)
    biasH = ar.alloc(D, F32)
    WsT = ar.alloc(8 * P, BF16)
    sgcol = ar.alloc(1, F32)
    neglam = ar.alloc(1, F32)
    small = ar.alloc(64, F32)
    persist_mark = ar.mark()

    DMA(cst_f, consts, [], ["cst_f"], "k_cst")
    COPY("dve", ident, cst_f[:, 0:P], ["cst_f"], ["ident"])
    COPY("dve", maskneg, cst_f[:, P:2 * P], ["cst_f"], ["maskneg"])
    MEMSET("dve", zeros, 0.0, ["zeros"])
    MEMSET("dve", neghalf, -0.5, ["neghalf"])
    DMA(fgB, final_g[0, :].partition_broadcast(P), [], ["fgB"], "k_fg")

    def issue_casts(l):
        def cast_cols(cols, grp, close):
            for n_, c0 in enumerate(cols):
                last = close and n_ == len(cols) - 1
                DMA(wb_in[l, :, c0:c0 + 512], w_in[l, :, c0:c0 + 512], [], [("wb", l, grp)] if last else [],
                    ("cast", l, grp), q="pool", max_dma_last_dim=4096)

        def cast_rows(src, dst, grp, close):
            for c in range(2):
                last = close and c == 1
                DMA(dst[l, c * 512:(c + 1) * 512, :], src[l, c * 512:(c + 1) * 512, :], [],
                    [("wb", l, grp)] if last else [], ("cast", l, grp), q="pool", max_dma_last_dim=4096)

        cast_cols([C_U, C_U + 512, C_VA, C_VA + 512, C_ZA, C_ZA + 512, C_VB, C_VB + 512], 1, True)
        cast_cols([C_Q, C_Q + 512, C_K, C_K + 512, C_ZB, C_ZB + 512, C_GB, C_GB + 512], 2, True)
        cast_cols([C_GA, C_GA + 512], 3, False)
        cast_rows(w_a_out, wb_a, 3, True)
        cast_rows(w_b_out, wb_b, 4, False)
        cast_rows(w_o, wb_o, 4, False)
        cast_rows(w_ple_gate, wb_pg, 4, False)
        DMA(wb_ple[l], w_ple[l], [], [("wb", l, 4)], ("cast", l, 4), q="pool", max_dma_last_dim=4096)

    issue_casts(0)

    def psb(b0, nb=1):
        if nb == 1:
            return ps[:, b0, :]
        return ps[:, b0:b0 + nb, :].rearrange("p b n -> p (b n)")

    def psb16(b0):
        return ps[:, b0, :].bitcast(BF16)

    for l in range(NL):
        x_src = x_in if l == 0 else xs
        is_last = (l == NL - 1)
        li = lambda_init(l)

        T.barrier()
        ar.reset(persist_mark)
        pm = ar.mark()
        DMA(gB, norm_g[l, :].partition_broadcast(P), [], ["gB"], "k_gB")
        DMA(plgB, ple_norm_g[l, :].partition_broadcast(P), [], ["plgB"], "k_plgB")
        DMA(lngB, a_ln_g[l, :].partition_broadcast(P), [], ["lngB"], "k_lngB")
        TS("dve", lngB, lngB, 0.5, None, ALU.mult, None, ["lngB"], ["lngB"])
        lnbB = ar.alloc(D, F32)
        DMA(lnbB, a_ln_b[l, :].partition_broadcast(P), [], ["lnbB"], "k_lnbB")
        ws_tok = ar.alloc(8 * P, F32)
        ws3 = ws_tok.rearrange("p (g s) -> p g s", g=8)
        DMA(ws3, a_ws[l].rearrange("g t s -> t g s"), [], ["ws_tok"], "k_ws")
        bs_tok = small[:, 0:8]
        rw_tok = small[:, 8:16]
        DMA(bs_tok, a_bs[l].rearrange("g t -> t g"), [], ["bs_tok"], "k_bs", allow_slow_non_contiguous=True)
        for g in range(8):
            TT("dve", ws3[:, g, :], ws3[:, g, :], tril01, ALU.mult, ["ws_tok", "cst_f"], ["ws_tok"])
        OP("dve", "tensor_reduce", ["ws_tok"], ["rw_tok"], out=rw_tok, in_=ws3, op=ALU.add, axis=AX.X)
        TS("dve", rw_tok, rw_tok, 0.5, None, ALU.mult, None, ["rw_tok"], ["rw_tok"])
        TS("dve", bs_tok, bs_tok, 0.5, None, ALU.mult, None, ["bs_tok"], ["bs_tok"])
        for g in range(8):
            TS("dve", biasH[:, g * P:(g + 1) * P], lnbB[:, g * P:(g + 1) * P], rw_tok[:, g:g + 1],
               bs_tok[:, g:g + 1], ALU.mult, ALU.add, ["lnbB", "rw_tok", "bs_tok"], ["biasH"])
        ws_b = ar.alloc(8 * P, BF16)
        COPY("dve", ws_b, ws_tok, ["ws_tok"], ["ws_b"])
        for g in range(8):
            TR(psb16(0)[:, g * P:(g + 1) * P], ws_b[:, g * P:(g + 1) * P], ident, ["ws_b", "ident"], [("ps", 0)])
        COPY("dve", WsT, psb16(0), [("ps", 0)], ["WsT"])
        DMA(sgcol, subln_g[l, :].rearrange("(p o) -> p o", o=1), [], ["sgcol"], "k_sg")
        TS("dve", sgcol, sgcol, (1.0 - li) * 0.5, None, ALU.mult, None, ["sgcol"], ["sgcol"])
        lq = ar.alloc(P, F32)
        lk = ar.alloc(P, F32)
        DMA(lq, lam_q[l, :].partition_broadcast(P), [], ["lq"], "k_lq")
        DMA(lk, lam_k[l, :].partition_broadcast(P), [], ["lk"], "k_lk")
        TT("dve", lq, lq, lk, ALU.mult, ["lq", "lk"], ["lq"])
        lsum = small[:, 16:18]
        OP("dve", "tensor_reduce", ["lq"], ["lsum"], out=lsum, in_=lq.rearrange("p (m d) -> p m d", m=2),
           op=ALU.add, axis=AX.X)
        lexp = small[:, 18:20]
        ACT(lexp, lsum, AF.Exp, ["lsum"], ["lexp"])
        TT("dve", neglam, lexp[:, 1:2], lexp[:, 0:1], ALU.subtract, ["lexp"], ["neglam"])
        TS("dve", neglam, neglam, -li, None, ALU.add, None, ["neglam"], ["neglam"])
        ar.reset(pm)

        T.barrier()
        ar.reset(persist_mark)
        NTB = TBA // P
        NHH = TBA // 512
        NWS = 8
        wslots = [ar.alloc(8 * 512, BF16) for _ in range(NWS)]
        hT = ar.alloc(8 * TBA, BF16)
        hT3 = hT.rearrange("p (c t) -> p c t", c=8)
        aT = ar.alloc(8 * TBA, BF16)
        aT3 = aT.rearrange("p (c t) -> p c t", c=8)
        xt = [ar.alloc(D, F32) for _ in range(2)]
        junk = ar.alloc(D, BF16)
        xn = [ar.alloc(D, BF16) for _ in range(2)]
        ssA = ar.alloc(4, F32)
        gu = [ar.alloc(D, BF16) for _ in range(2)]
        thz = [ar.alloc(D, BF16) for _ in range(2)]
        s2z = [ar.alloc(D, BF16) for _ in range(2)]
        gv = [ar.alloc(D, F32) for _ in range(2)]
        vhat = [ar.alloc(D, BF16) for _ in range(2)]
        t1 = [ar.alloc(D, F32) for _ in range(2)]
        a_tok = [ar.alloc(D, BF16) for _ in range(2)]
        stats = ar.alloc(32, F32)
        stg = [ar.alloc(512, BF16) for _ in range(4)]
        tht = [ar.alloc(512, BF16) for _ in range(2)]
        vst = [ar.alloc(D, BF16) for _ in range(2)]

        wstate = {"n": 0}

        def load_w(src3, col0):
            s = wstate["n"] % NWS
            wstate["n"] += 1
            w3 = wslots[s].rearrange("p (c n) -> p c n", c=8)
            if src3 is wa3:
                grp = 3
            elif col0 in (C_U, C_U + 512, C_VA, C_VA + 512, C_ZA, C_ZA + 512, C_VB, C_VB + 512):
                grp = 1
            elif col0 in (C_GA, C_GA + 512):
                grp = 3
            else:
                grp = 2
            DMA(w3, src3[:, :, col0:col0 + 512], [("wb", l, grp)], [("wsl", s)], ("wsl", s))
            return s, w3

        win3 = wb_in[l].rearrange("(c p) n -> p c n", p=P)
        wa3 = wb_a[l].rearrange("(c p) n -> p c n", p=P)

        pj = {"n": 0}

        def pj_bank():
            b = 2 + (pj["n"] % 6)
            pj["n"] += 1
            return b

        def pj_pair():
            if pj["n"] % 2 == 1:
                pj["n"] += 1
            b = 2 + (pj["n"] % 6)
            pj["n"] += 2
            return b

        trn = {"n": 0}

        def tr_bank():
            b = trn["n"] % 2
            trn["n"] += 1
            return b

        cnt = {"x": 0, "tok": 0, "stg": 0, "tht": 0, "vst": 0}

        ssAr = [ar.alloc(4, F32) for _ in range(3)]
        statr = [ar.alloc(20, F32) for _ in range(2)]
        xt3 = xt + [ar.alloc(D, F32)]

        def hres(hh):
            return [("hT", 4 * hh + t) for t in range(4)]

        prefetched = {}
        for blk in range(S // TBA):
            t0 = blk * TBA
            W = {}
            W["wu"] = prefetched.pop("wu") if "wu" in prefetched else [load_w(win3, C_U + i * 512) for i in range(2)]
            W["wv"] = prefetched.pop("wv") if "wv" in prefetched else [load_w(win3, C_VA + i * 512) for i in range(2)]

            xslot = {}

            def T1a(j, t0=t0):
                xi = cnt["x"] % 3
                ni = cnt["x"] % 2
                cnt["x"] += 1
                xslot[j] = ni
                xtile = xt3[xi]
                sA = ssAr[xi]
                DMA(xtile, x_src[t0 + j * P:t0 + (j + 1) * P, :], [], [("xt", xi)], ("xt", xi))
                ACT(junk, xtile, AF.Square, [("xt", xi)], ["junk", ("ssA", xi)], accum=sA[:, 0:1])
                rsqrt_pool(sA[:, 1:2], sA[:, 0:1], 1.0 / D, EPS, sA[:, 2:3], [("ssA", xi)], [("rstdA", xi)],
                           ("tmpA", xi))
                STT(xn[ni], xtile, sA[:, 1:2], gB, ALU.mult, ALU.mult, [("xt", xi), ("rstdA", xi), "gB"],
                    [("xn", ni)])

            def T1b(j):
                ni = xslot[j]
                tb = tr_bank()
                for c in range(8):
                    TR(psb16(tb)[:, c * P:(c + 1) * P], xn[ni][:, c * P:(c + 1) * P], ident,
                       [("xn", ni), "ident"], [("ps", tb)])
                COPY("act", hT3[:, :, j * P:(j + 1) * P], psb16(tb).rearrange("p (c t) -> p c t", c=8),
                     [("ps", tb)], [("hT", j)])

            def tokproj(wpair, j):
                tsl = slice(j * P, (j + 1) * P)
                b = pj_pair()
                for hh in range(2):
                    sw, w3 = wpair[hh]
                    for k in range(8):
                        MM(ps[:, b + hh, :], hT3[:, k, tsl], w3[:, k, :], k == 0, k == 7,
                           [("hT", j), ("wsl", sw)], [("ps", b + hh)])
                return b

            def T2(j):
                ti = j % 2
                stt = statr[ti]
                bu = tokproj(W["wu"], j)
                ACT(gu[ti], psb(bu, 2), AF.Gelu, [("ps", bu), ("ps", bu + 1)], [("gu", ti)])
                bv = tokproj(W["wv"], j)
                ACT(gv[ti], psb(bv, 2), AF.Gelu, [("ps", bv), ("ps", bv + 1)], [("gv", ti)])
                for hh in range(2):
                    OP("dve", "bn_stats", [("gv", ti)], [("st6", ti)], out=stt[:, hh * 6:(hh + 1) * 6],
                       in_=gv[ti][:, hh * 512:(hh + 1) * 512])
                OP("dve", "bn_aggr", [("st6", ti)], [("mv", ti)], out=stt[:, 12:14], in_=stt[:, 0:12])
                rsqrt_pool(stt[:, 14:15], stt[:, 13:14], 1.0, EPS, stt[:, 15:16], [("mv", ti)], [("rstdV", ti)],
                           ("tmpV", ti))
                bz = tokproj(W["wz"], j)
                ACT(thz[ti], psb(bz, 2), AF.Tanh, [("ps", bz), ("ps", bz + 1)], [("thz", ti)], scale=0.5)
                STT(stt[:, 16:17], stt[:, 12:13], -1.0, stt[:, 14:15], ALU.mult, ALU.mult,
                    [("mv", ti), ("rstdV", ti)], [("nmr", ti)])
                TS("dve", vhat[ti], gv[ti], stt[:, 14:15], stt[:, 16:17], ALU.mult, ALU.add,
                   [("gv", ti), ("rstdV", ti), ("nmr", ti)], [("vhat", ti)])
                STT(s2z[ti], thz[ti], 1.0, psb(bz, 2), ALU.add, ALU.mult,
                    [("thz", ti), ("ps", bz), ("ps", bz + 1)], [("s2z", ti)])
                TT("dve", gu[ti], gu[ti], s2z[ti], ALU.mult, [("gu", ti), ("s2z", ti)], [("gu", ti)])

            def T5(j, t0=t0):
                b = tokproj(W["wvb"], j)
                vi = cnt["vst"] % 2
                cnt["vst"] += 1
                COPY("act", vst[vi], psb(b, 2), [("ps", b), ("ps", b + 1)], [("vst", vi)])
                DMA(VS[t0 + j * P:t0 + (j + 1) * P, :], vst[vi], [("vst", vi)], [], ("vst", vi))

            def T3(j):
                ti = j % 2
                by = pj_pair()
                for g in range(8):
                    MM(ps[:, by + g // 4, (g % 4) * P:(g % 4 + 1) * P], WsT[:, g * P:(g + 1) * P],
                       vhat[ti][:, g * P:(g + 1) * P], True, True, ["WsT", ("vhat", ti)], [("ps", by + g // 4)])
                TT("dve", t1[ti], psb(by, 2), lngB, ALU.mult, [("ps", by), ("ps", by + 1), "lngB"], [("t1", ti)])
                TT("pool", t1[ti], t1[ti], biasH, ALU.add, [("t1", ti), "biasH"], [("t1", ti)])
                TT("dve", a_tok[ti], t1[ti], gu[ti], ALU.mult, [("t1", ti), ("gu", ti)], [("a_tok", ti)])

            def T4(j):
                ti = j % 2
                tb = tr_bank()
                for c in range(8):
                    TR(psb16(tb)[:, c * P:(c + 1) * P], a_tok[ti][:, c * P:(c + 1) * P], ident,
                       [("a_tok", ti), "ident"], [("ps", tb)])
                COPY("act", aT3[:, :, j * P:(j + 1) * P], psb16(tb).rearrange("p (c t) -> p c t", c=8),
                     [("ps", tb)], [("aT", j)])

            seq = [("q", C_Q), ("k", C_K), ("zb", C_ZB), ("gb", C_GB)]
            order = [(nm, c0, i) for nm, c0 in seq for i in range(2)]
            pending = {}

            def s4_load(oi):
                if oi < len(order):
                    nm, c0, i = order[oi]
                    if (nm, i) not in pending:
                        pending[(nm, i)] = load_w(win3, c0 + i * 512)

            def s4_item(oi, t0=t0):
                nm, c0, i = order[oi]
                s4_load(oi + 1)
                s4_load(oi + 2)
                sw, w3 = pending.pop((nm, i))
                dst = {"q": QT, "k": KT, "zb": ZBs, "gb": THB}[nm]
                for ctl in range(4):
                    ct = i * 4 + ctl
                    for hh in range(NHH):
                        b = pj_bank()
                        for k in range(8):
                            MM(ps[:, b, :], w3[:, k, ctl * P:(ctl + 1) * P], hT3[:, k, hh * 512:(hh + 1) * 512],
                               k == 0, k == 7, hres(hh) + [("wsl", sw)], [("ps", b)])
                        si = cnt["stg"] % 4
                        cnt["stg"] += 1
                        if nm in ("q", "k"):
                            COPY("act", stg[si], ps[:, b, :], [("ps", b)], [("stg", si)])
                        elif nm == "gb":
                            ACT(stg[si], ps[:, b, :], AF.Tanh, [("ps", b)], [("stg", si)], scale=0.5)
                        else:
                            hi = cnt["tht"] % 2
                            cnt["tht"] += 1
                            ACT(tht[hi], ps[:, b, :], AF.Tanh, [("ps", b)], [("tht", hi)], scale=0.5)
                            STT(stg[si], tht[hi], 1.0, ps[:, b, :], ALU.add, ALU.mult,
                                [("tht", hi), ("ps", b)], [("stg", si)])
                        DMA(dst[ct, :, t0 + hh * 512:t0 + (hh + 1) * 512], stg[si], [("stg", si)], [],
                            ("stg", si), q="pool")

            T1a(0)
            T1b(0)
            for it in range(1, NTB + 1):
                if it < NTB:
                    T1a(it)
                if it == 1:
                    W["wz"] = [load_w(win3, C_ZA + i * 512) for i in range(2)]
                    W["wvb"] = [load_w(win3, C_VB + i * 512) for i in range(2)]
                T2(it - 1)
                if it < NTB:
                    T1b(it)
                T5(it - 1)
                if it - 2 >= 0:
                    T3(it - 2)
                if it - 3 >= 0:
                    T4(it - 3)
            s4_load(0)
            s4_load(1)
            T3(NTB - 1)
            T4(NTB - 2)
            s4_item(0)
            T4(NTB - 1)
            for oi in range(1, len(order)):
                s4_item(oi)
            s5w = [(load_w(win3, C_GA + i * 512), load_w(wa3, i * 512)) for i in range(2)]
            if blk + 1 < S // TBA:
                prefetched["wu"] = [load_w(win3, C_U + i * 512) for i in range(2)]
                prefetched["wv"] = [load_w(win3, C_VA + i * 512) for i in range(2)]
            for i in range(2):
                (sg, wg3), (sa, wa_3) = s5w[i]
                for ctl in range(4):
                    ct = i * 4 + ctl
                    for hh in range(NHH):
                        bg = pj_bank()
                        for k in range(8):
                            MM(ps[:, bg, :], wg3[:, k, ctl * P:(ctl + 1) * P], hT3[:, k, hh * 512:(hh + 1) * 512],
                               k == 0, k == 7, hres(hh) + [("wsl", sg)], [("ps", bg)])
                        hi = cnt["tht"] % 2
                        cnt["tht"] += 1
                        ACT(tht[hi], ps[:, bg, :], AF.Tanh, [("ps", bg)], [("tht", hi)], scale=0.5)
                        by = pj_bank()
                        for k in range(8):
                            MM(ps[:, by, :], wa_3[:, k, ctl * P:(ctl + 1) * P], aT3[:, k, hh * 512:(hh + 1) * 512],
                               k == 0, k == 7, [("aT", 4 * hh + t) for t in range(4)] + [("wsl", sa)], [("ps", by)])
                        si = cnt["stg"] % 4
                        cnt["stg"] += 1
                        STT(stg[si], tht[hi], 1.0, ps[:, by, :], ALU.add, ALU.mult,
                            [("tht", hi), ("ps", by)], [("stg", si)])
                        DMA(GAYA[ct, :, t0 + hh * 512:t0 + (hh + 1) * 512], stg[si], [("stg", si)], [],
                            ("stg", si), q="pool")

        T.barrier()
        if l + 1 < NL:
            issue_casts(l + 1)
        ar.reset(persist_mark)
        wbS = ar.alloc(8 * D, BF16)
        woS = ar.alloc(8 * D, BF16)
        wpgS = ar.alloc(8 * D, BF16)
        wplS = ar.alloc(2 * D, BF16)
        wbS3 = wbS.rearrange("p (c n) -> p c n", c=8)
        woS3 = woS.rearrange("p (c n) -> p c n", c=8)
        wpgS3 = wpgS.rearrange("p (c n) -> p c n", c=8)
        wplS3 = wplS.rearrange("p (c n) -> p c n", c=2)
        cw_mark = ar.mark()
        HB = []
        for i in range(2):
            HB.append({
                "kt": ar.alloc(S, BF16), "qa": ar.alloc(S, BF16), "qb": ar.alloc(S, BF16),
                "v": ar.alloc(NT * 130, BF16), "zb": ar.alloc(S, BF16),
            })
        QB = 256
        NQB2 = S // QB
        DEFER = 12
        pT = [ar.alloc(2 * QB, BF16) for _ in range(3)]
        o_f = [ar.alloc(P, F32) for _ in range(6)]
        o_junk = ar.alloc(P, F32)
        on_b = [ar.alloc(P, BF16) for _ in range(20)]
        ost = [ar.alloc(QB, BF16) for _ in range(10)]
        rsB = [ar.alloc(8, F32) for _ in range(6)]
        oS = [ar.alloc(260, F32) for _ in range(6)]
        for i in range(2):
            MEMSET("dve", HB[i]["qa"][64:128, :], 0.0, [("qa", i)])
            MEMSET("dve", HB[i]["qb"][0:64, :], 0.0, [("qb", i)])
            v3 = HB[i]["v"].rearrange("p (k e) -> p k e", e=130)
            MEMSET("dve", v3[:, :, 128:130], 1.0, [("v", i)])

        def load_head(h):
            i = h % 2
            hb = HB[i]
            DMA(hb["kt"], KT[h], [], [("kt", i)], ("hk", i))
            DMA(hb["qa"][0:64, :], QT[h, 0:64, :], [], [("qa", i)], ("hqa", i))
            DMA(hb["qb"][64:128, :], QT[h, 64:128, :], [], [("qb", i)], ("hqb", i))
            v3 = hb["v"].rearrange("p (k e) -> p k e", e=130)
            DMA(v3[:, :, 0:128], VS.rearrange("(k p) c -> p k c", p=P)[:, :, h * P:(h + 1) * P],
                [], [("v", i)], ("hv", i))
            DMA(hb["zb"], ZBs[h], [], [("zb", i)], ("hz", i))

        def obank(g, qi):
            return 3 + 2 * (g % 2) + qi

        def oacc(g, m, qi):
            return ps[:, obank(g, qi), m * 130:m * 130 + 129]

        qorder = []
        lo, hi_ = 0, NQB2 - 1
        while lo <= hi_:
            qorder.append(hi_)
            if lo != hi_:
                qorder.append(lo)
            lo += 1
            hi_ -= 1
        steps = [(h, Qb, kt) for h in range(NHB) for Qb in qorder for kt in range(2 * Qb + 2)]
        NS = len(steps)
        gof = {}
        for (h_, Qb_, kt_) in steps:
            gof.setdefault((h_, Qb_), len(gof))
        head_first = {}
        for n, (h, Qb, kt) in enumerate(steps):
            head_first.setdefault(h, n)

        def emit_S(n):
            h, Qb, kt = steps[n]
            i = h % 2
            hb = HB[i]
            jd = kt - 2 * Qb
            c0 = max(jd, 0) * P
            sl = n % 3
            qsl = slice(Qb * QB + c0, Qb * QB + QB)
            for m in range(2):
                qsrc = hb["qa"] if m == 0 else hb["qb"]
                qres = ("qa", i) if m == 0 else ("qb", i)
                if jd >= 0:
                    MM(ps[:, sl, m * QB + c0:m * QB + c0 + P], ident, maskneg, m == 0, False,
                       ["ident", "maskneg"], [("S", sl)])
                    MM(ps[:, sl, m * QB + c0:(m + 1) * QB], hb["kt"][:, kt * P:(kt + 1) * P], qsrc[:, qsl], False, True,
                       [("kt", i), qres], [("S", sl)])
                else:
                    MM(ps[:, sl, m * QB:(m + 1) * QB], hb["kt"][:, kt * P:(kt + 1) * P], qsrc[:, qsl], m == 0, True,
                       [("kt", i), qres], [("S", sl)])

        def emit_EXP(n):
            h, Qb, kt = steps[n]
            c0 = max(kt - 2 * Qb, 0) * P
            sl = n % 3
            pi = n % 3
            p3 = pT[pi].rearrange("p (m q) -> p m q", m=2)
            s3 = ps[:, sl, :].rearrange("p (m q) -> p m q", m=2)
            ACT(p3[:, :, c0:QB], s3[:, :, c0:QB], AF.Exp, [("S", sl)], [("pT", pi)], scale=0.125)

        def emit_PV(n):
            h, Qb, kt = steps[n]
            i = h % 2
            hb = HB[i]
            g = gof[(h, Qb)]
            v3 = hb["v"].rearrange("p (k e) -> p k e", e=130)
            jd = kt - 2 * Qb
            pi = n % 3
            p3 = pT[pi].rearrange("p (m q) -> p m q", m=2)
            for qi in range(max(jd, 0), 2):
                if kt == 0:
                    MM(ps[:, obank(g, qi), :], zeros, hb["kt"][:, 0:512], True, True, ["zeros", ("kt", i)],
                       [("Ob", obank(g, qi))])
                for m in range(2):
                    MM(oacc(g, m, qi), p3[:, m, qi * P:(qi + 1) * P], v3[:, kt, 0:129], False, kt == 2 * Qb + qi,
                       [("pT", pi), ("v", i)], [("Ob", obank(g, qi))])

        evs = {"n": 0}
        deferred = []

        def run_deferred(upto):
            while deferred and deferred[0][0] <= upto:
                deferred.pop(0)[1]()

        def defer(due, fn, head):
            k = len(deferred)
            while k > 0 and deferred[k - 1][0] > due:
                k -= 1
            deferred.insert(k, (due, fn, head))

        fin_due = {"last": -100}

        def emit_evac(n):
            h, Qb, kt = steps[n]
            i = h % 2
            hb = HB[i]
            g = gof[(h, Qb)]
            qi = kt - 2 * Qb
            ob = obank(g, qi)
            oi = g % 10
            en = evs["n"]
            evs["n"] += 1
            ei = en % 6
            bi = en % 20
            rs = rsB[ei]
            ki = en % 6
            osb = oS[ki]
            COPY("dve", osb[:, 0:259], ps[:, ob, 0:259], [("Ob", ob)], [("osb", ki)])
            OP("dve", "reciprocal", [("osb", ki)], [("rs", ei)], out=rs[:, 0:2], in_=osb[:, 128:259:130])
            TT("dve", rs[:, 2:3], rs[:, 1:2], neglam, ALU.mult, [("rs", ei), "neglam"], [("rs2", ei)])
            TS("dve", o_f[ei], osb[:, 0:P], rs[:, 0:1], None, ALU.mult, None,
               [("osb", ki), ("rs", ei)], [("o_f", ei)])
            STT(o_f[ei], osb[:, 130:130 + P], rs[:, 2:3], o_f[ei], ALU.mult, ALU.add,
                [("osb", ki), ("rs2", ei), ("o_f", ei)], [("o_f", ei)])
            OP("dve", "scalar_tensor_tensor", [("o_f", ei)], ["o_junk", ("ssq", ei)], out=o_junk,
               in0=o_f[ei], scalar=1.0, in1=o_f[ei], op0=ALU.mult, op1=ALU.mult, accum_out=rs[:, 3:4])
            rsqrt_pool(rs[:, 4:5], rs[:, 3:4], 1.0 / P, EPS, rs[:, 5:6], [("ssq", ei)], [("rstdO", ei)],
                       ("tmpO", ei))

            def part2():
                TS("dve", on_b[bi], o_f[ei], rs[:, 4:5], None, ALU.mult, None,
                   [("o_f", ei), ("rstdO", ei)], [("on_b", bi)])

            def finish():
                tro = psb16(7)[:, (en % 8) * P:(en % 8 + 1) * P]
                TR(tro, on_b[bi], ident, [("on_b", bi), "ident"], [("ps", 7)])
                STT(ost[oi][:, qi * P:(qi + 1) * P], tro, sgcol[:, 0:1],
                    hb["zb"][:, Qb * QB + qi * P:Qb * QB + (qi + 1) * P], ALU.mult, ALU.mult,
                    [("ps", 7), "sgcol", ("zb", i)], [("ost", oi)])
                if qi == 1:
                    DMA(OBT[h, :, Qb * QB:(Qb + 1) * QB], ost[oi], [("ost", oi)], [], ("ost", oi))

            defer(n + 4, part2, h)
            due = max(n + DEFER, fin_due["last"] + 7)
            fin_due["last"] = due
            defer(due, finish, h)

        load_head(0)
        loaded = {0}
        emit_S(0)
        emit_S(1)
        for n in range(NS):
            h, Qb, kt = steps[n]
            if n + 2 < NS:
                emit_S(n + 2)
            emit_EXP(n)
            emit_PV(n)
            run_deferred(n)
            if h + 1 < NHB and (h + 1) not in loaded and n >= head_first[h] + 2 \
                    and all(d[2] >= h for d in deferred):
                load_head(h + 1)
                loaded.add(h + 1)
            if h + 1 < NHB and (h + 1) not in loaded and n >= head_first[h + 1] - 6:
                run_deferred(NS * 10)
                load_head(h + 1)
                loaded.add(h + 1)
            if n == min(40, NS - 1):
                for c2 in range(2):
                    DMA(wbS3[:, :, c2 * 512:(c2 + 1) * 512],
                        wb_b[l].rearrange("(c p) n -> p c n", p=P)[:, :, c2 * 512:(c2 + 1) * 512],
                        [("wb", l, 4)], ["wbS"], "cw0")
                for c2 in range(2):
                    DMA(woS3[:, :, c2 * 512:(c2 + 1) * 512],
                        wb_o[l].rearrange("(c p) n -> p c n", p=P)[:, :, c2 * 512:(c2 + 1) * 512],
                        [("wb", l, 4)], ["woS"], "cw1")
                for c2 in range(2):
                    DMA(wpgS3[:, :, c2 * 512:(c2 + 1) * 512],
                        wb_pg[l].rearrange("(c p) n -> p c n", p=P)[:, :, c2 * 512:(c2 + 1) * 512],
                        [("wb", l, 4)], ["wpgS"], "cw2")
                DMA(wplS3, wb_ple[l].rearrange("(c p) n -> p c n", p=P), [("wb", l, 4)], ["wplS"], "cw3")
            if kt - 2 * Qb >= 0:
                emit_evac(n)
        run_deferred(NS * 10)

        T.barrier()
        ar.reset(persist_mark)
        ar.reset(cw_mark)
        obB = [ar.alloc(8 * 512, BF16) for _ in range(2)]
        thB = [ar.alloc(8 * 512, BF16) for _ in range(2)]
        gyB = [ar.alloc(8 * 512, BF16) for _ in range(2)]
        mgT = ar.alloc(8 * 512, BF16)
        mgT3 = mgT.rearrange("p (c t) -> p c t", c=8)
        m1 = [ar.alloc(512, BF16) for _ in range(2)]
        xc = [ar.alloc(D, F32) for _ in range(2)]
        x1 = [ar.alloc(D, F32) for _ in range(2)]
        junkC = ar.alloc(D, BF16)
        xn1 = [ar.alloc(D, BF16) for _ in range(2)]
        hpT = [ar.alloc(8 * P, BF16) for _ in range(2)]
        thp = [ar.alloc(D, BF16) for _ in range(2)]
        pt_f = [ar.alloc(PLE, F32) for _ in range(2)]
        pt_b = [ar.alloc(PLE, BF16) for _ in range(2)]
        pTt = [ar.alloc(2 * P, BF16) for _ in range(2)]
        tC = [ar.alloc(D, F32) for _ in range(2)]
        x2 = [ar.alloc(D, F32) for _ in range(1)]
        yo = [ar.alloc(D, F32) for _ in range(2)]
        ssC = [ar.alloc(8, F32) for _ in range(2)]

        pjc = {"n": 0}

        def pjc_pair():
            b = 2 + 2 * (pjc["n"] % 3)
            pjc["n"] += 1
            return b

        trc = {"n": 0}

        def trc_bank():
            b = trc["n"] % 2
            trc["n"] += 1
            return b

        mgT_b = ar.alloc(8 * 512, BF16)
        mgTs = [mgT3, mgT_b.rearrange("p (c t) -> p c t", c=8)]
        x1r = x1 + [x2[0], ar.alloc(D, F32)]
        ssCr = ssC + [ar.alloc(8, F32), ar.alloc(8, F32)]
        obt3 = OBT.rearrange("c p t -> p c t")
        thb3 = THB.rearrange("c p t -> p c t")
        gya3 = GAYA.rearrange("c p t -> p c t")
        NBLK = S // 512

        def YB(blk):
            t0 = blk * 512
            bi = blk % 2
            ob3 = obB[bi].rearrange("p (c t) -> p c t", c=8)
            th3 = thB[bi].rearrange("p (c t) -> p c t", c=8)
            gy3 = gyB[bi].rearrange("p (c t) -> p c t", c=8)
            DMA(ob3, obt3[:, :, t0:t0 + 512], [], [("obB", bi)], ("obB", bi))
            DMA(th3, thb3[:, :, t0:t0 + 512], [], [("thB", bi)], ("thB", bi))
            DMA(gy3, gya3[:, :, t0:t0 + 512], [], [("gyB", bi)], ("gyB", bi))
            for ct in range(8):
                b = pjc_pair()
                for k in range(8):
                    MM(ps[:, b, :], wbS3[:, k, ct * P:(ct + 1) * P], ob3[:, k, :], k == 0, k == 7,
                       ["wbS", ("obB", bi)], [("ps", b)])
                mi = ct % 2
                STT(m1[mi], th3[:, ct, :], 1.0, ps[:, b, :], ALU.add, ALU.mult, [("thB", bi), ("ps", b)], [("m1", mi)])
                TT("dve", mgTs[bi][:, ct, :], m1[mi], gy3[:, ct, :], ALU.add, [("m1", mi), ("gyB", bi)],
                   [("mgT", bi)])

        def CA(g):
            blk, jt = divmod(g, 4)
            bi = blk % 2
            ti = g % 2
            xi = g % 4
            tok0 = g * P
            DMA(xc[ti], x_src[tok0:tok0 + P, :], [], [("xc", ti)], ("xc", ti))
            DMA(pt_f[ti], p_in[l, tok0:tok0 + P, :], [], [("pt_f", ti)], ("ptf", ti))
            b = pjc_pair()
            for hh in range(2):
                for k in range(8):
                    MM(ps[:, b + hh, :], mgTs[bi][:, k, jt * P:(jt + 1) * P], woS3[:, k, hh * 512:(hh + 1) * 512],
                       k == 0, k == 7, [("mgT", bi), "woS"], [("ps", b + hh)])
            STT(x1r[xi], psb(b, 2), 0.5, xc[ti], ALU.mult, ALU.add, [("ps", b), ("ps", b + 1), ("xc", ti)],
                [("x1", xi)])
            sc = ssCr[xi]
            ACT(junkC, x1r[xi], AF.Square, [("x1", xi)], ["junkC", ("ssc", xi)], accum=sc[:, 0:1])
            rsqrt_pool(sc[:, 1:2], sc[:, 0:1], 1.0 / D, EPS, sc[:, 2:3], [("ssc", xi)], [("rstdC", xi)],
                       ("tmpC", xi))
            STT(xn1[ti], x1r[xi], sc[:, 1:2], plgB, ALU.mult, ALU.mult, [("x1", xi), ("rstdC", xi), "plgB"],
                [("xn1", ti)])
            COPY("dve", pt_b[ti], pt_f[ti], [("pt_f", ti)], [("pt_b", ti)])

        def CB1(g):
            ti = g % 2
            tb = trc_bank()
            for c in range(8):
                TR(psb16(tb)[:, c * P:(c + 1) * P], xn1[ti][:, c * P:(c + 1) * P], ident,
                   [("xn1", ti), "ident"], [("ps", tb)])
            COPY("act", hpT[ti], psb16(tb), [("ps", tb)], [("hpT", ti)])
            tb = trc_bank()
            for c in range(2):
                TR(psb16(tb)[:, c * P:(c + 1) * P], pt_b[ti][:, c * P:(c + 1) * P], ident,
                   [("pt_b", ti), "ident"], [("ps", tb)])
            COPY("act", pTt[ti], psb16(tb)[:, 0:2 * P], [("ps", tb)], [("pTt", ti)])

        def CB2(g):
            ti = g % 2
            xi = g % 4
            tok0 = g * P
            sc = ssCr[xi]
            hp3 = hpT[ti].rearrange("p (c t) -> p c t", c=8)
            b = pjc_pair()
            for hh in range(2):
                for k in range(8):
                    MM(ps[:, b + hh, :], hp3[:, k, :], wpgS3[:, k, hh * 512:(hh + 1) * 512], k == 0, k == 7,
                       [("hpT", ti), "wpgS"], [("ps", b + hh)])
            ACT(thp[ti], psb(b, 2), AF.Tanh, [("ps", b), ("ps", b + 1)], [("thp", ti)], scale=0.5)
            pT3 = pTt[ti].rearrange("p (c t) -> p c t", c=2)
            b = pjc_pair()
            for hh in range(2):
                for k in range(2):
                    MM(ps[:, b + hh, :], pT3[:, k, :], wplS3[:, k, hh * 512:(hh + 1) * 512], k == 0, k == 1,
                       [("pTt", ti), "wplS"], [("ps", b + hh)])
            STT(tC[ti], thp[ti], 1.0, psb(b, 2), ALU.add, ALU.mult, [("thp", ti), ("ps", b), ("ps", b + 1)],
                [("tC", ti)])
            STT(x1r[xi], tC[ti], 0.5, x1r[xi], ALU.mult, ALU.add, [("tC", ti), ("x1", xi)], [("x1", xi)])
            if not is_last:
                DMA(xs[tok0:tok0 + P, :], x1r[xi], [("x1", xi)], [], ("x2", xi), q="pool")
            elif not last_final:
                DMA(out[tok0:tok0 + P, :], x1r[xi], [("x1", xi)], [], ("x2", xi), q="pool")
            else:
                ACT(junkC, x1r[xi], AF.Square, [("x1", xi)], ["junkC", ("ssf", xi)], accum=sc[:, 3:4])
                rsqrt_pool(sc[:, 4:5], sc[:, 3:4], 1.0 / D, EPS, sc[:, 5:6], [("ssf", xi)], [("rstdF", xi)],
                           ("tmpF", xi))
                STT(yo[ti], x1r[xi], sc[:, 4:5], fgB, ALU.mult, ALU.mult, [("x1", xi), ("rstdF", xi), "fgB"],
                    [("yo", ti)])
                DMA(out[tok0:tok0 + P, :], yo[ti], [("yo", ti)], [], ("yo", ti), q="pool")

        YB(0)
        for it in range(NT + 2):
            if it < NT:
                CA(it)
                if it % 4 == 1 and it // 4 + 1 < NBLK:
                    YB(it // 4 + 1)
            if 0 <= it - 1 < NT:
                CB1(it - 1)
            if 0 <= it - 2 < NT:
                CB2(it - 2)

    T.barrier()
    T.finalize()
    return nc, T


def make_consts():
    c = np.zeros((P, 4 * P), np.float32)
    c[:, 0:P] = np.eye(P, dtype=np.float32)
    k = np.arange(P)[:, None]
    q = np.arange(P)[None, :]
    c[:, P:2 * P] = np.where(q < k, -30000.0, 0.0)
    c[:, 2 * P:3 * P] = np.where(q <= k, 1.0, 0.0)
    c[:, 3 * P:4 * P] = 1.0
    return c


_PROG_CACHE = {}


def _get_program(S, NL):
    key = (S, NL)
    if key not in _PROG_CACHE:
        _PROG_CACHE[key] = build_program(S, NL)
    return _PROG_CACHE[key][0]


def make_in_maps(x, p, norm_g, w_in, a_ln_g, a_ln_b, a_ws, a_bs, lam_q, lam_k, subln_g,
                 w_a_out, w_b_out, w_o, ple_norm_g, w_ple, w_ple_gate, final_g, n_cores):
    f = lambda a: np.ascontiguousarray(np.asarray(a, dtype=np.float32))
    NL = w_in.shape[0]
    shared = {
        "norm_g": f(norm_g), "w_in": f(w_in), "a_ln_g": f(a_ln_g), "a_ln_b": f(a_ln_b),
        "a_ws": f(a_ws), "a_bs": f(a_bs), "lam_q": f(lam_q).reshape(NL, P), "lam_k": f(lam_k).reshape(NL, P),
        "subln_g": f(subln_g), "w_a_out": f(w_a_out), "w_b_out": f(w_b_out), "w_o": f(w_o),
        "ple_norm_g": f(ple_norm_g), "w_ple": f(w_ple), "w_ple_gate": f(w_ple_gate),
        "final_g": f(final_g).reshape(1, D), "consts": make_consts(),
    }
    x = f(x)
    p = f(p)
    maps = []
    for c in range(n_cores):
        m = dict(shared)
        m["x"] = np.ascontiguousarray(x[c])
        m["p"] = np.ascontiguousarray(p[:, c])
        maps.append(m)
    return maps


def kernel(x, p, norm_g, w_in, a_ln_g, a_ln_b, a_ws, a_bs, lam_q, lam_k, subln_g,
           w_a_out, w_b_out, w_o, ple_norm_g, w_ple, w_ple_gate, final_g):
    x = np.asarray(x)
    B, S, _ = x.shape
    NL = np.asarray(w_in).shape[0]
    assert B == N_CORES
    nc = _get_program(S, NL)
    maps = make_in_maps(x, p, norm_g, w_in, a_ln_g, a_ln_b, a_ws, a_bs, lam_q, lam_k, subln_g,
                        w_a_out, w_b_out, w_o, ple_norm_g, w_ple, w_ple_gate, final_g, N_CORES)
    res = run_bass_kernel_spmd(nc, maps, core_ids=list(range(N_CORES)))
    return np.stack([np.asarray(r["out"], dtype=np.float32) for r in res.results], axis=0)
```
